# Optimizing a Trainium2 kernel written in Bass

```python
import numpy as np
import jax, jax.numpy as jnp
from jax import lax

D_MODEL = 2048
BATCH = 4
SEQ = 8192
DEPTH = 1
DEC_BATCH = 1
DEC_SEQ = 8192
PAST_LEN = 128

N_META = 16
GRID_W = 64
WIN_ROWS = 8
WIN_COLS = 16
N_Q_HEADS = 32
N_KV_HEADS = 8
Q_PER_KV = N_Q_HEADS // N_KV_HEADS
HEAD_DIM = 64
ATTN_W = N_Q_HEADS * HEAD_DIM
KV_W = N_KV_HEADS * HEAD_DIM
SSD_INNER = 2 * D_MODEL
SSD_HEAD_DIM = 64
SSD_HEADS = SSD_INNER // SSD_HEAD_DIM
SSD_GROUPS = 8
SSD_HEADS_PER_GROUP = SSD_HEADS // SSD_GROUPS
SSD_STATE = 128
D_CONV = 4
CHUNK = 128
CONV_DIM = SSD_INNER + 2 * SSD_GROUPS * SSD_STATE
IN_W = ATTN_W + 2 * KV_W + SSD_INNER + CONV_DIM + 2 * SSD_HEADS
MIX_W = ATTN_W + SSD_INNER
D_FF = -(-(-(-8 * D_MODEL // 3)) // 256) * 256
EPS = 1e-6

kernel_name = "hymba_na2d_bissd_encoder"


def rms_norm(x, g):
    xf = x.astype(jnp.float32)
    y = xf * lax.rsqrt(jnp.mean(xf * xf, axis=-1, keepdims=True) + EPS)
    return (y * g.astype(jnp.float32)).astype(x.dtype)


def neighbourhood_attention(q, k, v, k_meta, v_meta, rpb):
    b, n = q.shape[:2]
    rows = n // GRID_W
    kh = min(WIN_ROWS, rows)
    qg = q.reshape(b, rows, GRID_W, N_KV_HEADS, Q_PER_KV, HEAD_DIM)
    kg = k.reshape(b, rows, GRID_W, N_KV_HEADS, HEAD_DIM)
    vg = v.reshape(b, rows, GRID_W, N_KV_HEADS, HEAD_DIM)
    cols = np.arange(GRID_W)
    col_start = np.clip(cols - WIN_COLS // 2, 0, GRID_W - WIN_COLS)
    col_idx = (col_start[:, None] + np.arange(WIN_COLS)[None, :]).astype(np.int32)
    dc = (col_idx - cols[:, None] + WIN_COLS - 1).astype(np.int32)
    rpb_g = rpb.astype(jnp.float32).reshape(N_KV_HEADS, Q_PER_KV, 2 * WIN_ROWS - 1, 2 * WIN_COLS - 1)
    scale = HEAD_DIM ** -0.5
    n_win = kh * WIN_COLS

    def row_block(r):
        rs = jnp.clip(r - kh // 2, 0, rows - kh)
        q_r = lax.dynamic_index_in_dim(qg, r, axis=1, keepdims=False)
        k_band = lax.dynamic_slice_in_dim(kg, rs, kh, axis=1)
        v_band = lax.dynamic_slice_in_dim(vg, rs, kh, axis=1)
        k_win = k_band[:, :, col_idx]
        v_win = v_band[:, :, col_idx]
        dr = rs + jnp.arange(kh) - r + WIN_ROWS - 1
        bias = rpb_g[:, :, dr[None, :, None], dc[:, None, :]]
        s_win = jnp.einsum('bckrd,bicjkd->bkrcij', q_r, k_win).astype(jnp.float32) * scale + bias
        s_meta = jnp.einsum('bckrd,bmkd->bkrcm', q_r, k_meta).astype(jnp.float32) * scale
        s = jnp.concatenate([s_win.reshape(b, N_KV_HEADS, Q_PER_KV, GRID_W, n_win), s_meta], axis=-1)
        p = jax.nn.softmax(s, axis=-1).astype(v.dtype)
        p_win = p[..., :n_win].reshape(b, N_KV_HEADS, Q_PER_KV, GRID_W, kh, WIN_COLS)
        p_meta = p[..., n_win:]
        return (jnp.einsum('bkrcij,bicjkd->bckrd', p_win, v_win)
                + jnp.einsum('bkrcm,bmkd->bckrd', p_meta, v_meta))

    out = lax.map(row_block, jnp.arange(rows))
    return jnp.moveaxis(out, 0, 1).reshape(b, n, ATTN_W)


def meta_attention(q_meta, k_meta, v_meta):
    b = q_meta.shape[0]
    s = jnp.einsum('bmkrd,bnkd->bkrmn', q_meta, k_meta).astype(jnp.float32) * HEAD_DIM ** -0.5
    p = jax.nn.softmax(s, axis=-1).astype(v_meta.dtype)
    return jnp.einsum('bkrmn,bnkd->bmkrd', p, v_meta).reshape(b, N_META, ATTN_W)


def ssd_scan(x, dt, a, bm, cm):
    b, t = x.shape[:2]
    nc = t // CHUNK
    g, r = SSD_GROUPS, SSD_HEADS_PER_GROUP

    def to_chunks(z):
        return jnp.moveaxis(z.reshape((b, nc, CHUNK) + z.shape[2:]), 1, 0)

    xs = to_chunks(x.astype(jnp.float32) * dt[..., None]).reshape(nc, b, CHUNK, g, r, SSD_HEAD_DIM)
    da = to_chunks(dt * a).reshape(nc, b, CHUNK, g, r)
    bs = to_chunks(bm.astype(jnp.float32))
    cs = to_chunks(cm.astype(jnp.float32))
    lower = np.tril(np.ones((CHUNK, CHUNK), dtype=bool))[None, :, :, None, None]

    def step(state, inp):
        xc, dac, bc, cc = inp
        acs = jnp.cumsum(dac, axis=1)
        seg = acs[:, :, None] - acs[:, None, :]
        lmat = jnp.exp(jnp.where(lower, seg, -jnp.inf))
        cb = jnp.einsum('blgn,bsgn->blsg', cc, bc)
        y = jnp.einsum('blsg,blsgr,bsgrp->blgrp', cb, lmat, xc)
        y = y + jnp.einsum('blgn,bgrpn->blgrp', cc, state) * jnp.exp(acs)[..., None]
        last = acs[:, -1]
        decay_in = jnp.exp(last[:, None] - acs)
        state = (state * jnp.exp(last)[..., None, None]
                 + jnp.einsum('bsgn,bsgr,bsgrp->bgrpn', bc, decay_in, xc))
        return state, y

    init = jnp.zeros((b, g, r, SSD_HEAD_DIM, SSD_STATE), jnp.float32)
    _, ys = lax.scan(step, init, (xs, da, bs, cs))
    return jnp.moveaxis(ys, 0, 1).reshape(b, t, SSD_HEADS, SSD_HEAD_DIM)


def pad_front(a, p):
    return jnp.pad(a, [(0, 0), (p, 0)] + [(0, 0)] * (a.ndim - 2))


def ssd_mixer(z, xbc, dt_raw, conv_w, conv_b, dt_bias_f, dt_bias_b, a_log_f, a_log_b, d_skip, norm_w):
    b, t = xbc.shape[:2]
    left = D_CONV // 2
    xbc = lax.conv_general_dilated(
        xbc, conv_w.reshape(D_CONV, 1, CONV_DIM).astype(xbc.dtype), window_strides=(1,),
        padding=[(left, D_CONV - 1 - left)], dimension_numbers=('NWC', 'WIO', 'NWC'),
        feature_group_count=CONV_DIM)
    xbc = jax.nn.silu(xbc + conv_b)
    xs = xbc[..., :SSD_INNER].reshape(b, t, SSD_HEADS, SSD_HEAD_DIM)
    bm = xbc[..., SSD_INNER:SSD_INNER + SSD_GROUPS * SSD_STATE].reshape(b, t, SSD_GROUPS, SSD_STATE)
    cm = xbc[..., SSD_INNER + SSD_GROUPS * SSD_STATE:].reshape(b, t, SSD_GROUPS, SSD_STATE)
    dtf = dt_raw.astype(jnp.float32)
    dt_f = jax.nn.softplus(dtf[..., :SSD_HEADS] + dt_bias_f.astype(jnp.float32))
    dt_b = jax.nn.softplus(dtf[..., SSD_HEADS:] + dt_bias_b.astype(jnp.float32))
    a_f = -jnp.exp(a_log_f.astype(jnp.float32))
    a_b = -jnp.exp(a_log_b.astype(jnp.float32))
    p = CHUNK - N_META
    xs_p, bm_p, cm_p = pad_front(xs, p), pad_front(bm, p), pad_front(cm, p)
    y_f = ssd_scan(xs_p, pad_front(dt_f, p), a_f, bm_p, cm_p)
    flip = lambda u: jnp.flip(u, axis=1)
    y_b = flip(ssd_scan(flip(xs_p), flip(pad_front(dt_b, p)), a_b, flip(bm_p), flip(cm_p)))
    y = (y_f + y_b)[:, p:] + xs.astype(jnp.float32) * d_skip.astype(jnp.float32)[:, None]
    y = y.reshape(b, t, SSD_INNER) * jax.nn.silu(z.astype(jnp.float32))
    y = y.reshape(b, t, SSD_GROUPS, SSD_INNER // SSD_GROUPS)
    y = y * lax.rsqrt(jnp.mean(y * y, axis=-1, keepdims=True) + EPS)
    return (y.reshape(b, t, SSD_INNER) * norm_w.astype(jnp.float32)).astype(z.dtype)


def encoder_layer(h, g_mix, w_in, q_norm, k_norm, rpb, conv_w, conv_b, dt_bias_f, dt_bias_b,
                  a_log_f, a_log_b, d_skip, ssd_norm, w_out, g_ffn, w_gate, w_up, w_down):
    b, t, _ = h.shape
    u = rms_norm(h, g_mix)
    proj = u @ w_in
    o1 = ATTN_W
    o2 = o1 + KV_W
    o3 = o2 + KV_W
    o4 = o3 + SSD_INNER
    o5 = o4 + CONV_DIM
    q, k, v, z, xbc, dt_raw = jnp.split(proj, [o1, o2, o3, o4, o5], axis=-1)
    q = rms_norm(q.reshape(b, t, N_KV_HEADS, Q_PER_KV, HEAD_DIM), q_norm)
    k = rms_norm(k.reshape(b, t, N_KV_HEADS, HEAD_DIM), k_norm)
    v = v.reshape(b, t, N_KV_HEADS, HEAD_DIM)
    k_meta, v_meta = k[:, :N_META], v[:, :N_META]
    attn_real = neighbourhood_attention(q[:, N_META:], k[:, N_META:], v[:, N_META:], k_meta, v_meta, rpb)
    attn_meta = meta_attention(q[:, :N_META], k_meta, v_meta)
    attn = jnp.concatenate([attn_meta, attn_real], axis=1).astype(h.dtype)
    ssd = ssd_mixer(z, xbc, dt_raw, conv_w, conv_b, dt_bias_f, dt_bias_b, a_log_f, a_log_b, d_skip, ssd_norm)
    h = h + jnp.concatenate([attn, ssd], axis=-1) @ w_out
    f = rms_norm(h, g_ffn)
    return h + (jax.nn.silu(f @ w_gate) * (f @ w_up)) @ w_down


def run_trunk(x, meta_tokens, g_mix, w_in, q_norm, k_norm, rpb, conv_w, conv_b, dt_bias_f, dt_bias_b,
              a_log_f, a_log_b, d_skip, ssd_norm, w_out, g_ffn, w_gate, w_up, w_down):
    b = x.shape[0]
    meta = jnp.broadcast_to(meta_tokens.astype(x.dtype)[None], (b, N_META, D_MODEL))
    h = jnp.concatenate([meta, x], axis=1)
    for l in range(DEPTH):
        h = encoder_layer(h, g_mix[l], w_in[l], q_norm[l], k_norm[l], rpb[l], conv_w[l], conv_b[l],
                          dt_bias_f[l], dt_bias_b[l], a_log_f[l], a_log_b[l], d_skip[l], ssd_norm[l],
                          w_out[l], g_ffn[l], w_gate[l], w_up[l], w_down[l])
    return h[:, N_META:]


def setup_inputs(seed: int = 0) -> dict:
    key = jax.random.key(seed)
    ks = jax.random.split(key, 24)
    f32 = jnp.float32
    nrm = lambda k, s, sc: jax.random.normal(k, s, f32) * sc
    dt0 = jnp.exp(jax.random.uniform(ks[10], (2, DEPTH, SSD_HEADS), f32)
                  * (jnp.log(0.1) - jnp.log(0.001)) + jnp.log(0.001))
    dt_bias = dt0 + jnp.log(-jnp.expm1(-dt0))
    a_log = jnp.log(jax.random.uniform(ks[11], (2, DEPTH, SSD_HEADS), f32, 1.0, 16.0))
    return {
        "x_prompt": nrm(ks[0], (BATCH, SEQ, D_MODEL), 1.0),
        "x_sample": nrm(ks[1], (DEC_BATCH, DEC_SEQ, D_MODEL), 1.0),
        "meta_tokens": nrm(ks[2], (N_META, D_MODEL), 1.0),
        "g_mix": 1.0 + nrm(ks[3], (DEPTH, D_MODEL), 0.02),
        "w_in": nrm(ks[4], (DEPTH, D_MODEL, IN_W), D_MODEL ** -0.5),
        "q_norm": 1.0 + nrm(ks[5], (DEPTH, HEAD_DIM), 0.02),
        "k_norm": 1.0 + nrm(ks[6], (DEPTH, HEAD_DIM), 0.02),
        "rpb": nrm(ks[7], (DEPTH, N_Q_HEADS, 2 * WIN_ROWS - 1, 2 * WIN_COLS - 1), 0.02),
        "conv_w": nrm(ks[8], (DEPTH, D_CONV, CONV_DIM), D_CONV ** -0.5),
        "conv_b": nrm(ks[9], (DEPTH, CONV_DIM), 0.02),
        "dt_bias_f": dt_bias[0],
        "dt_bias_b": dt_bias[1],
        "a_log_f": a_log[0],
        "a_log_b": a_log[1],
        "d_skip": 1.0 + nrm(ks[12], (DEPTH, SSD_HEADS), 0.02),
        "ssd_norm": 1.0 + nrm(ks[13], (DEPTH, SSD_INNER), 0.02),
        "w_out": nrm(ks[14], (DEPTH, MIX_W, D_MODEL), MIX_W ** -0.5),
        "g_ffn": 1.0 + nrm(ks[15], (DEPTH, D_MODEL), 0.02),
        "w_gate": nrm(ks[16], (DEPTH, D_MODEL, D_FF), D_MODEL ** -0.5),
        "w_up": nrm(ks[17], (DEPTH, D_MODEL, D_FF), D_MODEL ** -0.5),
        "w_down": nrm(ks[18], (DEPTH, D_FF, D_MODEL), D_FF ** -0.5),
    }


def reference(x_prompt, x_sample, meta_tokens, g_mix, w_in, q_norm, k_norm, rpb, conv_w, conv_b,
              dt_bias_f, dt_bias_b, a_log_f, a_log_b, d_skip, ssd_norm, w_out, g_ffn, w_gate, w_up, w_down):
    y_prompt = run_trunk(x_prompt, meta_tokens, g_mix, w_in, q_norm, k_norm, rpb, conv_w, conv_b,
                         dt_bias_f, dt_bias_b, a_log_f, a_log_b, d_skip, ssd_norm, w_out, g_ffn,
                         w_gate, w_up, w_down)
    y_sample = run_trunk(x_sample, meta_tokens, g_mix, w_in, q_norm, k_norm, rpb, conv_w, conv_b,
                         dt_bias_f, dt_bias_b, a_log_f, a_log_b, d_skip, ssd_norm, w_out, g_ffn,
                         w_gate, w_up, w_down)
    return (y_prompt, y_sample)
```

```python
from contextlib import ExitStack
import numpy as np
import concourse.bass as bass
import concourse.mybir as mybir
from concourse.bass_utils import run_bass_kernel_spmd

F32 = mybir.dt.float32
BF16 = mybir.dt.bfloat16
AF = mybir.ActivationFunctionType
ALU = mybir.AluOpType

D = 2048
NMETA = 16
GW = 64
DFF = 5632
EPS = 1e-6
SELF_WAIT = True


class Buf:
    def __init__(self, name="", multi=False):
        self.w = {}
        self.r = {}
        self.name = name
        self.multi = multi
        self.dsem = None
        self.dcnt = 0


class K:
    def __init__(self, nc, stack):
        self.nc = nc
        self.stack = stack
        self.E = {"pe": nc.tensor, "act": nc.scalar, "dve": nc.vector, "pool": nc.gpsimd, "sp": nc.sync}
        self.sems = []
        self.esem = {}
        self.ecnt = {}
        for e in ["pe", "act", "dve", "pool"]:
            self.esem[e] = self.new_sem("e_" + e)
            self.ecnt[e] = 0
        self.seen = {e: {} for e in self.E}
        self.smax = {}

    def barrier(self):
        for e in self.E:
            for si, v in self.smax.items():
                if v <= 0 or self.seen[e].get(si, 0) >= v:
                    continue
                if e == "pe" and si == self.esem["pe"]:
                    continue
                self.E[e].wait_ge(self.sems[si], v)
                self.seen[e][si] = v

    def new_sem(self, name):
        h = self.stack.enter_context(self.nc.semaphore(name + "_%d" % len(self.sems)))
        self.sems.append(h)
        return len(self.sems) - 1

    def _waits(self, e, r, w, noself=False):
        need = {}
        for b in r:
            for si, v in b.w.items():
                need[si] = max(need.get(si, 0), v)
        for b in w:
            if not b.multi:
                for si, v in b.w.items():
                    need[si] = max(need.get(si, 0), v)
            for si, v in b.r.items():
                need[si] = max(need.get(si, 0), v)
        for si, v in need.items():
            if e in self.esem and si == self.esem[e]:
                if e == "pe" or not SELF_WAIT or noself:
                    continue
            if self.seen[e].get(si, 0) >= v:
                continue
            self.E[e].wait_ge(self.sems[si], v)
            self.seen[e][si] = v

    def op(self, e, fn, r=(), w=(), sig=True, noself=False):
        self._waits(e, r, w, noself)
        ins = fn(self.E[e])
        si = self.esem[e]
        if sig:
            self.ecnt[e] += 1
            ins.then_inc(self.sems[si], 1)
            tok = self.ecnt[e]
            self.smax[si] = tok
        else:
            assert e == "pe"
            tok = self.ecnt[e] + 1
        for b in r:
            b.r[si] = max(b.r.get(si, 0), tok)
        for b in w:
            if b.multi:
                b.w[si] = max(b.w.get(si, 0), tok)
            else:
                b.w = {si: tok}
                b.r = {}
        return ins

    def dma(self, q, out, in_, slot, r=(), w=()):
        self._waits(q, r, w)
        if slot.dsem is None:
            slot.dsem = self.new_sem("d_" + slot.name)
        ins = self.E[q].dma_start(out=out, in_=in_)
        slot.dcnt += 16
        ins.then_inc(self.sems[slot.dsem], 16)
        tok = slot.dcnt
        si = slot.dsem
        self.smax[si] = tok
        for b in r:
            b.r[si] = max(b.r.get(si, 0), tok)
        for b in w:
            if b.multi:
                b.w[si] = max(b.w.get(si, 0), tok)
            else:
                b.w = {si: tok}
                b.r = {}
        return ins

    def final_wait(self, bufs):
        need = {}
        for b in bufs:
            for si, v in b.w.items():
                need[si] = max(need.get(si, 0), v)
        for si, v in need.items():
            self.E["sp"].wait_ge(self.sems[si], v)


class Stream:
    def __init__(self, loads, depth):
        self.loads = loads
        self.nx = 0
        self.depth = depth

    def need(self, i):
        lim = min(i + self.depth, len(self.loads) - 1)
        while self.nx <= lim:
            self.loads[self.nx]()
            self.nx += 1


class Psum:
    def __init__(self, nc, stack):
        self.t = [stack.enter_context(nc.psum_tensor("psd%d" % i, [128, 1024], F32)) for i in range(4)]
        self.b = [Buf("ps%d" % i) for i in range(8)]
        self.i = 0

    def one(self):
        i = self.i
        self.i = (self.i + 1) % 8
        t = self.t[i // 2]
        return t[:, (i % 2) * 512:(i % 2) * 512 + 512], self.b[i]

    def two(self):
        if self.i % 2:
            self.i = (self.i + 1) % 8
        i = self.i
        self.i = (self.i + 2) % 8
        return self.t[i // 2][:, :], self.b[i], self.b[i + 1]


def bc(ap, shape):
    return ap.to_broadcast(shape)


def build(NCH):
    NCHP = NCH + 1
    LP = 128 * NCHP
    LPX = LP + 4
    NT = NCH * 128
    assert NCH >= 5
    nc = bass.Bass("TRN2", target_bir_lowering=False)
    dt_ = nc.dram_tensor

    def din(name, shape, dtype=F32):
        return dt_(name, list(shape), dtype, kind="ExternalInput").ap()

    def dsc(name, shape, dtype):
        return dt_(name, list(shape), dtype).ap()

    xin = din("xin", [LP, D])
    w_fm = din("w_fm", [68, 128, 16 * 128])
    w_tm = din("w_tm", [9, 128, 16 * 512])
    w_dt = din("w_dt", [128, 16 * 128])
    w_out = din("w_out", [24, 128, 8 * 512])
    w_gu = din("w_gu", [44, 128, 2 * 16 * 128])
    w_dn = din("w_dn", [44, 128, 4 * 512])
    gmix_i = din("gmix", [128, 16])
    gffn_i = din("gffn", [128, 16])
    qkg_i = din("qkg", [128, 2])
    rbl = din("rbl", [5, 32, 128, 640])
    msk = din("msk", [5, 128, 640])
    cw_i = din("cw", [128, 4 * 48])
    cb_i = din("cb", [1, 6144])
    dtb_i = din("dtb", [1, 128])
    alog_i = din("alog", [1, 128])
    dsk_i = din("dsk", [1, 64])
    nw_i = din("nw", [1, 4096])
    cst_i = din("cst", [128, 7 * 128])
    yout = dt_("y", [NT, D], F32, kind="ExternalOutput").ap()

    wb_fm = dsc("wb_fm", [68, 128, 2048], BF16)
    wb_tm = dsc("wb_tm", [9, 128, 8192], BF16)
    wb_dt = dsc("wb_dt", [128, 2048], BF16)
    wb_out = dsc("wb_out", [24, 128, 4096], BF16)
    wb_gu = dsc("wb_gu", [44, 128, 4096], BF16)
    wb_dn = dsc("wb_dn", [44, 128, 2048], BF16)
    qT = dsc("qT", [64, 32, LP], BF16)
    kT = dsc("kT", [64, 8, LP], BF16)
    vS = dsc("vS", [LP, 512], BF16)
    xbcT = dsc("xbcT", [48, 128, LPX], BF16)
    szS = dsc("szS", [LP, 4096], BF16)
    dtS = dsc("dtS", [LP, 128], F32)
    mixS = dsc("mixS", [LP, 6144], BF16)
    yfS = dsc("yfS", [LP, 4096], F32)
    ebf = dsc("ebf", [5, 128, 32, 640], BF16)
    cbD = dsc("cbD", [2, 6144], BF16)
    xsS = dsc("xsS", [LP, 4096], BF16)
    BtS = dsc("BtS", [LP, 1024], BF16)
    BCTS = dsc("BCTS", [NCHP, 128, 2048], BF16)

    B_wb = {n: Buf(n, multi=True) for n in ["fm", "tm", "dt", "out", "gu", "dn"]}
    B_qT, B_kT, B_vS, B_xbc, B_sz, B_dt, B_mix, B_yf, B_ebf, B_y = [
        Buf(n, multi=True) for n in ["qT", "kT", "vS", "xbc", "sz", "dtS", "mix", "yf", "ebf", "y"]]

    with ExitStack() as stack:
        k = K(nc, stack)
        PS = Psum(nc, stack)
        block = stack.enter_context(nc.Block())

        def sb(name, shape, dtype, st=None):
            return (st or stack).enter_context(nc.sbuf_tensor("s_" + name, list(shape), dtype))

        @block.sync
        def _(sync):
            cst = sb("cst", [128, 7, 128], F32)
            cstb = sb("cstb", [128, 7, 128], BF16)
            Bc = Buf("cst")
            k.dma("sp", cst[:].rearrange("p a b -> p (a b)"), cst_i, Bc, w=[Bc])
            k.op("dve", lambda e: e.tensor_copy(out=cstb[:], in_=cst[:]), r=[Bc], w=[Bc])
            ident = cstb[:, 0, :]
            TRI_LE, TRI_GE, SGT, SLT, ONESF, BDb = cst[:, 1, :], cst[:, 2, :], cst[:, 3, :], cst[:, 4, :], cst[:, 5, :], cstb[:, 6, :]
            MASKF, MASKB = cst[:, 1, :], cst[:, 2, :]
            epsT = sb("epsT", [128, 2], F32)
            k.op("dve", lambda e: e.memset(epsT[:, 0:1], EPS), w=[Bc])
            k.op("dve", lambda e: e.memset(epsT[:, 1:2], 1.0), w=[Bc])
            eps_ap, one_ap = epsT[:, 0:1], epsT[:, 1:2]
            gmix = sb("gmixt", [128, 16], F32)
            gffn = sb("gffnt", [128, 16], F32)
            qkg = sb("qkgt", [128, 2], F32)
            k.dma("sp", gmix[:], gmix_i, Bc, w=[Bc])
            k.dma("sp", gffn[:], gffn_i, Bc, w=[Bc])
            k.dma("sp", qkg[:], qkg_i, Bc, w=[Bc])
            k.op("dve", lambda e: e.tensor_scalar(out=qkg[:, 0:1], in0=qkg[:, 0:1], scalar1=0.125, scalar2=None, op0=ALU.mult), r=[Bc], w=[Bc])

            Bp0s = {n_: Buf("p0" + n_) for n_ in ["fm", "tm", "dt", "out", "gu", "dn"]}
            k.dma("pool", wb_dt, w_dt, Bp0s["dt"], w=[B_wb["dt"]])
            for (src, dst, key, n) in [(w_fm, wb_fm, "fm", 68), (w_tm, wb_tm, "tm", 9)]:
                for a in range(n):
                    k.dma("pool", dst[a], src[a], Bp0s[key], w=[B_wb[key]])

            def mk_cast(src, dst, key, a):
                return lambda: k.dma("pool", dst[a], src[a], Bp0s[key], w=[B_wb[key]])
            cast_tasks = [mk_cast(src, dst, key, a) for (src, dst, key, n) in [(w_out, wb_out, "out", 24), (w_gu, wb_gu, "gu", 44), (w_dn, wb_dn, "dn", 44)] for a in range(n)]

            def late_casts():
                while cast_tasks:
                    cast_tasks.pop(0)()

            with ExitStack() as s1:
                xb = [sb("xb%d" % i, [128, D], F32, s1) for i in range(2)]
                xbB = [Buf("xb%d" % i) for i in range(2)]
                junk = sb("junk", [128, D], BF16, s1)
                Bjunk = Buf("junk")
                xn = [sb("xn%d" % i, [128, D], BF16, s1) for i in range(2)]
                xnB = [Buf("xn%d" % i) for i in range(2)]
                st1 = sb("st1", [128, 8], F32, s1)
                Bst1 = [Buf("st1a"), Buf("st1b")]
                uT = [sb("uT%d" % i, [128, 16, 512], BF16, s1) for i in range(2)]
                uTB = [Buf("uT%d" % i) for i in range(2)]
                wfm = [sb("wfm%d" % i, [128, 16, 128], BF16, s1) for i in range(3)]
                wfmB = [Buf("wfm%d" % i) for i in range(3)]
                wtm = [sb("wtm%d" % i, [128, 16, 512], BF16, s1) for i in range(2)]
                wtmB = [Buf("wtm%d" % i) for i in range(2)]
                wdt = sb("wdt", [128, 16, 128], BF16, s1)
                Bwdt = Buf("wdt")
                k.dma("sp", wdt[:].rearrange("p a b -> p (a b)"), wb_dt, Bwdt, r=[B_wb["dt"]], w=[Bwdt])
                qf = [sb("qf%d" % i, [128, 512], F32, s1) for i in range(2)]
                qfB = [Buf("qf%d" % i) for i in range(2)]
                sq = [sb("sq%d" % i, [128, 512], BF16, s1) for i in range(2)]
                sqB = [Buf("sq%d" % i) for i in range(2)]
                rs = [sb("rs%d" % i, [128, 512], F32, s1) for i in range(2)]
                rsB = [Buf("rs%d" % i) for i in range(2)]
                ob = [sb("ob%d" % i, [128, 512], BF16, s1) for i in range(4)]
                obB = [Buf("ob%d" % i) for i in range(4)]
                of = [sb("of%d" % i, [128, 128], F32, s1) for i in range(2)]
                ofB = [Buf("of%d" % i) for i in range(2)]
                zt = sb("zt", [128, 4], BF16, s1)
                Bzt = Buf("zt")
                k.op("dve", lambda e: e.memset(zt[:], 0.0), w=[Bzt])
                k.dma("sp", xbcT[:, :, 0:2].rearrange("c p t -> p c t"), bc(zt[:, 0:2].unsqueeze(1), [128, 48, 2]), Bzt, r=[Bzt], w=[B_xbc])
                k.dma("sp", xbcT[:, :, LP + 2:LP + 4].rearrange("c p t -> p c t"), bc(zt[:, 0:2].unsqueeze(1), [128, 48, 2]), Bzt, r=[Bzt], w=[B_xbc])
                obi = 0
                ntile = (NCHP + 3) // 4
                et = [sb("et%d" % i, [128, 640], F32, s1) for i in range(2)]
                etB = [Buf("et%d" % i) for i in range(2)]
                mt = sb("mt", [128, 5, 640], F32, s1)
                Bmt = Buf("mt")
                k.dma("sp", mt[:], msk.rearrange("v p q -> p v q"), Bmt, w=[Bmt])
                eo = [sb("eo%d" % i, [128, 640], BF16, s1) for i in range(2)]
                eoB = [Buf("eo%d" % i) for i in range(2)]

                def mk_e(it):
                    var, h = it // 32, it % 32
                    i2 = it % 2

                    def f():
                        k.dma("sp", et[i2][:], rbl[var, h], etB[i2], w=[etB[i2]])
                        k.op("act", lambda e: e.activation(out=et[i2][:], in_=et[i2][:], func=AF.Exp), r=[etB[i2]], w=[etB[i2]])
                        k.op("dve", lambda e: e.tensor_tensor(out=eo[i2][:], in0=et[i2][:], in1=mt[:, var, :], op=ALU.mult), r=[etB[i2], Bmt], w=[eoB[i2]])
                        k.dma("sp", ebf[var, :, h, :], eo[i2][:], eoB[i2], r=[eoB[i2]], w=[B_ebf])
                    return f
                e_tasks = [mk_e(it) for it in range(160)]
                e_gap = max(1, ((ntile - 1) * 68) // 161)
                c_gap = max(1, ((ntile - 2) * 68) // 113)
                xn4 = [sb("xnq%d" % i, [128, D], BF16, s1) for i in range(2)]
                xn_ = xn + xn4
                xnB_ = xnB + [Buf("xnq%d" % i) for i in range(2)]
                st1b = sb("st1b", [128, 16], F32, s1)
                Bst4 = [Buf("st1_%d" % i) for i in range(4)]

                def mk_x(gi):
                    return lambda: k.dma("sp", xb[gi % 2][:], xin[gi * 128:(gi + 1) * 128, :], xbB[gi % 2], w=[xbB[gi % 2]])
                x_st = Stream([mk_x(gi) for gi in range(NCHP)], 1)

                def mk_fm(i):
                    return lambda: k.dma("sp", wfm[i % 3][:].rearrange("p a b -> p (a b)"), wb_fm[i % 68], wfmB[i % 3], r=[B_wb["fm"]], w=[wfmB[i % 3]])
                fm_st = Stream([mk_fm(i) for i in range(ntile * 68)], 2)

                def mk_tm(i):
                    return lambda: k.dma("sp", wtm[i % 2][:].rearrange("p a b -> p (a b)"), wb_tm[i % 9], wtmB[i % 2], r=[B_wb["tm"]], w=[wtmB[i % 2]])
                tm_st = Stream([mk_tm(i) for i in range(ntile * 9)], 1)

                def norm_stage(j):
                    c0 = 4 * j
                    for ci in range(min(4, NCHP - c0)):
                        gi = c0 + ci
                        x_st.need(gi)
                        x_ = xb[gi % 2]
                        xB = xbB[gi % 2]
                        sB = Bst4[gi % 4]
                        so = 4 * (gi % 4)
                        k.op("dve", lambda e: e.memset(st1b[:, so:so + 1], 0.0), w=[sB])
                        k.op("act", lambda e: e.activation(out=junk[:], in_=x_[:], func=AF.Square, accum_out=st1b[:, so:so + 1]), r=[xB, sB], w=[Bjunk, sB])
                        k.op("act", lambda e: e.activation(out=st1b[:, so + 1:so + 2], in_=st1b[:, so:so + 1], func=AF.Ln, bias=eps_ap, scale=1.0 / D), r=[sB, Bc], w=[sB])
                        k.op("act", lambda e: e.activation(out=st1b[:, so + 2:so + 3], in_=st1b[:, so + 1:so + 2], func=AF.Exp, scale=-0.5), r=[sB], w=[sB])
                        k.op("dve", lambda e: e.tensor_scalar(out=xn_[gi % 4][:], in0=x_[:], scalar1=st1b[:, so + 2:so + 3], scalar2=None, op0=ALU.mult), r=[xB, sB], w=[xnB_[gi % 4]])

                def tr_stage(j):
                    c0 = 4 * j
                    u = uT[j % 2]
                    uB = uTB[j % 2]
                    for ci in range(min(4, NCHP - c0)):
                        gi = c0 + ci
                        for half in range(2):
                            pt, pB = PS.one()
                            ptb = pt.bitcast(BF16).rearrange("p (a b) -> p a b", b=128)
                            for q8 in range(8):
                                kc = half * 8 + q8
                                k.op("pe", lambda e: e.transpose(out=ptb[:, q8, :], in_=xn_[gi % 4][:, kc * 128:(kc + 1) * 128], identity=ident),
                                     r=[xnB_[gi % 4], Bc], w=[pB], sig=(q8 == 7))
                            k.op("dve", lambda e: e.tensor_tensor(out=u[:, half * 8:half * 8 + 8, ci * 128:(ci + 1) * 128], in0=ptb[:, 0:8, :],
                                                                  in1=bc(gmix[:, half * 8:half * 8 + 8].unsqueeze(2), [128, 8, 128]), op=ALU.mult),
                                 r=[pB, Bc], w=[uB])

                def make_rest(ct, i2, o_, oB, ps2pair, TW, pp0):
                    def rest():
                        ps2, pB2 = PS.one()
                        k.op("pe", lambda e: e.matmul(ps2[:, :TW], lhsT=BDb, rhs=sq[i2][:, :TW], start=True, stop=True), r=[sqB[i2], Bc], w=[pB2])
                        k.op("act", lambda e: e.activation(out=rs[i2][:, :TW], in_=ps2[:, :TW], func=AF.Ln, bias=eps_ap, scale=1.0 / 64), r=[pB2, Bc], w=[rsB[i2]])
                        k.op("act", lambda e: e.activation(out=rs[i2][:, :TW], in_=rs[i2][:, :TW], func=AF.Exp, scale=-0.5), r=[rsB[i2]], w=[rsB[i2]])
                        gcol = qkg[:, 0:1] if ct < 16 else qkg[:, 1:2]
                        k.op("dve", lambda e: e.scalar_tensor_tensor(out=o_[:, :TW], in0=qf[i2][:, :TW], scalar=gcol, in1=rs[i2][:, :TW], op0=ALU.mult, op1=ALU.mult),
                             r=[qfB[i2], rsB[i2], Bc], w=[oB])
                        for hh in range(2):
                            if ct < 16:
                                dst = qT[:, 2 * ct + hh, pp0:pp0 + TW]
                                dB = B_qT
                            else:
                                dst = kT[:, 2 * (ct - 16) + hh, pp0:pp0 + TW]
                                dB = B_kT
                            k.dma("sp", dst, o_[hh * 64:(hh + 1) * 64, :TW], oB, r=[oB], w=[dB])
                    return rest

                norm_stage(0)
                tr_stage(0)
                for j in range(ntile):
                    if j == ntile - 1:
                        while e_tasks:
                            e_tasks.pop(0)()
                    c0 = 4 * j
                    ncj = min(4, NCHP - c0)
                    TW = 128 * ncj
                    u = uT[j % 2]
                    uB = uTB[j % 2]
                    pp0 = c0 * 128
                    if j + 1 < ntile:
                        norm_stage(j + 1)
                    tm_st.need(j * 9 - 1)
                    deferred = None
                    for ct in range(68):
                        i = j * 68 + ct
                        fm_st.need(i)
                        wt = wfm[i % 3]
                        wB = wfmB[i % 3]
                        ps, pB = PS.one()
                        for kc in range(16):
                            k.op("pe", lambda e: e.matmul(ps[:, :TW], lhsT=wt[:, kc, :], rhs=u[:, kc, :TW], start=(kc == 0), stop=(kc == 15)),
                                 r=[wB, uB], w=[pB], sig=(kc == 15))
                        if deferred is not None:
                            deferred()
                            deferred = None
                        if e_tasks and (i % e_gap == 0):
                            e_tasks.pop(0)()

                        o_ = ob[obi % 4]
                        oB = obB[obi % 4]
                        obi += 1
                        if ct < 20:
                            i2 = ct % 2
                            k.op("act", lambda e: e.activation(out=qf[i2][:, :TW], in_=ps[:, :TW], func=AF.Copy), r=[pB], w=[qfB[i2]])
                            k.op("pool", lambda e: e.tensor_tensor(out=sq[i2][:, :TW], in0=qf[i2][:, :TW], in1=qf[i2][:, :TW], op=ALU.mult), r=[qfB[i2]], w=[sqB[i2]])
                            deferred = make_rest(ct, i2, o_, oB, None, TW, pp0)
                        else:
                            if ct % 2:
                                k.op("act", lambda e: e.activation(out=o_[:, :TW], in_=ps[:, :TW], func=AF.Copy), r=[pB], w=[oB])
                            else:
                                k.op("dve", lambda e: e.tensor_copy(out=o_[:, :TW], in_=ps[:, :TW]), r=[pB], w=[oB])
                            k.dma("sp", xbcT[ct - 20, :, 2 + pp0:2 + pp0 + TW], o_[:, :TW], oB, r=[oB], w=[B_xbc])
                    if deferred is not None:
                        deferred()
                        deferred = None
                    if j + 1 < ntile:
                        tr_stage(j + 1)
                    for ng in range(9):
                        i = j * 9 + ng
                        tm_st.need(i)
                        wt = wtm[i % 2]
                        wB = wtmB[i % 2]
                        for ci in range(ncj):
                            ps, pB = PS.one()
                            for kc in range(16):
                                k.op("pe", lambda e: e.matmul(ps[:, :], lhsT=u[:, kc, ci * 128:(ci + 1) * 128], rhs=wt[:, kc, :], start=(kc == 0), stop=(kc == 15)),
                                     r=[wB, uB], w=[pB], sig=(kc == 15))
                            o_ = ob[obi % 4]
                            oB = obB[obi % 4]
                            obi += 1
                            r0 = pp0 + ci * 128
                            if ng == 0:
                                k.op("dve", lambda e: e.tensor_copy(out=o_[:], in_=ps[:, :]), r=[pB], w=[oB])
                                k.dma("sp", vS[r0:r0 + 128, :], o_[:], oB, r=[oB], w=[B_vS])
                            else:
                                k.op("act", lambda e: e.activation(out=o_[:], in_=ps[:, :], func=AF.Silu), r=[pB], w=[oB])
                                k.dma("sp", szS[r0:r0 + 128, (ng - 1) * 512:ng * 512], o_[:], oB, r=[oB], w=[B_sz])
                    for ci in range(ncj):
                        ps, pB = PS.one()
                        for kc in range(16):
                            k.op("pe", lambda e: e.matmul(ps[:, :128], lhsT=u[:, kc, ci * 128:(ci + 1) * 128], rhs=wdt[:, kc, :], start=(kc == 0), stop=(kc == 15)),
                                 r=[Bwdt, uB], w=[pB], sig=(kc == 15))
                        o_ = of[ci % 2]
                        oB = ofB[ci % 2]
                        r0 = pp0 + ci * 128
                        k.op("dve", lambda e: e.tensor_copy(out=o_[:], in_=ps[:, :128]), r=[pB], w=[oB])
                        k.dma("sp", dtS[r0:r0 + 128, :], o_[:], oB, r=[oB], w=[B_dt])

            k.barrier()
            with ExitStack() as s2:
                NKT = NCH
                Eint = sb("Eint", [128, 32, 640], BF16, s2)
                BEint = Buf("Eint")
                Eed = sb("Eed", [128, 32, 640], BF16, s2)
                BEed = Buf("Eed")
                k.dma("sp", Eint[:], ebf[0], BEint, r=[B_ebf], w=[BEint])
                kmT = sb("kmT", [64, 8, 16], BF16, s2)
                vm = sb("vm", [16, 8, 65], BF16, s2)
                Bm = Buf("meta")
                k.op("dve", lambda e: e.memset(vm[:], 1.0), w=[Bm])
                k.dma("sp", kmT[:], kT[:, :, 112:128], Bm, r=[B_kT], w=[Bm])
                k.dma("sp", vm[:, :, 0:64], vS[112:128, :].rearrange("p (g d) -> p g d", d=64), Bm, r=[B_vS], w=[Bm])
                Kb = [sb("Kb%d" % i, [64, 8, 640], BF16, s2) for i in range(2)]
                Qb = [sb("Qb%d" % i, [64, 32, 128], BF16, s2) for i in range(2)]
                Vb = [sb("Vb%d" % i, [128, 5, 8, 65], BF16, s2) for i in range(2)]
                KbB = [Buf("Kb%d" % i) for i in range(2)]
                QbB = [Buf("Qb%d" % i) for i in range(2)]
                VbB = [Buf("Vb%d" % i) for i in range(2)]
                for i in range(2):
                    k.op("pool", lambda e: e.memset(Vb[i][:], 1.0), w=[VbB[i]])
                PT = [sb("PT%d" % i, [128, 5, 512], BF16, s2) for i in range(2)]
                PTB = [Buf("PT%d" % i) for i in range(2)]
                PM = [sb("PM%d" % i, [16, 512], BF16, s2) for i in range(2)]
                PMB = [Buf("PM%d" % i) for i in range(2)]
                rc = [sb("rc%d" % i, [128, 4], F32, s2) for i in range(2)]
                rcB = [Buf("rc%d" % i) for i in range(2)]
                Ot = [sb("Ot%d" % i, [128, 8, 4, 64], BF16, s2) for i in range(2)]
                OtB = [Buf("Ot%d" % i) for i in range(2)]
                def R_info(R):
                    KT0 = min(max(R - 2, 0), NKT - 5)
                    if R == 0:
                        var = 1
                    elif R == 1:
                        var = 2
                    elif R == NKT - 2:
                        var = 3
                    elif R == NKT - 1:
                        var = 4
                    else:
                        var = 0
                    return KT0, var

                def mk_R(R):
                    def f():
                        KT0, var = R_info(R)
                        b0 = 128 * (1 + KT0)
                        i2 = R % 2
                        k.dma("sp", Kb[i2][:], kT[:, :, b0:b0 + 640], KbB[i2], r=[B_kT], w=[KbB[i2]])
                        k.dma("sp", Qb[i2][:], qT[:, :, 128 * (1 + R):128 * (2 + R)], QbB[i2], r=[B_qT], w=[QbB[i2]])
                        for kt in range(5):
                            k.dma("sp", Vb[i2][:, kt, :, 0:64], vS[b0 + kt * 128:b0 + (kt + 1) * 128, :].rearrange("p (g d) -> p g d", d=64), VbB[i2], r=[B_vS], w=[VbB[i2]])
                    return f
                R_st = Stream([mk_R(R) for R in range(NKT)], 1)
                for R in range(NKT):
                    R_st.need(R)
                    for _ in range(2):
                        if cast_tasks:
                            cast_tasks.pop(0)()
                    KT0, var = R_info(R)
                    i2 = R % 2
                    if var == 0:
                        Ec, EB = Eint, BEint
                    else:
                        k.dma("sp", Eed[:], ebf[var], BEed, r=[B_ebf], w=[BEed])
                        Ec, EB = Eed, BEed
                    O_ = Ot[i2]
                    OB = OtB[i2]

                    def qk(g):
                        p2 = g % 2
                        for kt in range(5):
                            ps, pB = PS.one()
                            k.op("pe", lambda e: e.matmul(ps[:, :], lhsT=Kb[i2][:, g, kt * 128:(kt + 1) * 128], rhs=Qb[i2][:, 4 * g:4 * g + 4, :].rearrange("p h q -> p (h q)"), start=True, stop=True),
                                 r=[KbB[i2], QbB[i2]], w=[pB])
                            k.op("act", lambda e: e.activation(out=PT[p2][:, kt, :], in_=ps[:, :], func=AF.Exp), r=[pB], w=[PTB[p2]])
                            k.op("dve" if kt < 4 else "pool", lambda e: e.tensor_tensor(out=PT[p2][:, kt, :].rearrange("p (h q) -> p h q", q=128), in0=PT[p2][:, kt, :].rearrange("p (h q) -> p h q", q=128),
                                                                  in1=Ec[:, 4 * g:4 * g + 4, kt * 128:(kt + 1) * 128], op=ALU.mult), r=[PTB[p2], EB], w=[PTB[p2]])
                        ps, pB = PS.one()
                        k.op("pe", lambda e: e.matmul(ps[0:16, :], lhsT=kmT[:, g, :], rhs=Qb[i2][:, 4 * g:4 * g + 4, :].rearrange("p h q -> p (h q)"), start=True, stop=True),
                             r=[Bm, QbB[i2]], w=[pB])
                        k.op("act", lambda e: e.activation(out=PM[p2][:], in_=ps[0:16, :], func=AF.Exp), r=[pB], w=[PMB[p2]])

                    def pv(g):
                        p2 = g % 2
                        po, poB = PS.one()
                        for h in range(4):
                            for kt in range(5):
                                k.op("pe", lambda e: e.matmul(po[:, h * 65:(h + 1) * 65], lhsT=PT[p2][:, kt, h * 128:(h + 1) * 128], rhs=Vb[i2][:, kt, g, :], start=(kt == 0), stop=False),
                                     r=[PTB[p2], VbB[i2]], w=[poB], sig=False)
                            k.op("pe", lambda e: e.matmul(po[:, h * 65:(h + 1) * 65], lhsT=PM[p2][:, h * 128:(h + 1) * 128], rhs=vm[:, g, :], start=False, stop=True),
                                 r=[PMB[p2], Bm], w=[poB], sig=(h == 3))
                        po3 = po[:, 0:260].rearrange("p (h d) -> p h d", d=65)
                        k.op("dve", lambda e: e.reciprocal(out=rc[p2][:], in_=po3[:, :, 64]), r=[poB], w=[rcB[p2]])
                        k.op("dve", lambda e: e.tensor_tensor(out=O_[:, g, :, :], in0=po3[:, :, 0:64], in1=bc(rc[p2][:].unsqueeze(2), [128, 4, 64]), op=ALU.mult),
                             r=[poB, rcB[p2]], w=[OB])

                    qk(0)
                    for g in range(8):
                        if g + 1 < 8:
                            qk(g + 1)
                        pv(g)
                    r0 = 128 * (1 + R)
                    k.dma("sp", mixS[r0:r0 + 128, 0:2048], O_[:].rearrange("p g h d -> p (g h d)"), OB, r=[OB], w=[B_mix])
                late_casts()

            k.barrier()
            with ExitStack() as s3:
                dtb = sb("dtb", [128, 128], F32, s3)
                abc = sb("abc", [128, 128], F32, s3)
                dsk = sb("dsk", [128, 64], F32, s3)
                nwl = [sb("nwl%d" % i, [128, 512], F32, s3) for i in range(2)]
                nwlB = [Buf("nwl%d" % i) for i in range(2)]
                Bpar = Buf("par")
                k.dma("sp", dtb[:], dtb_i.partition_broadcast(128), Bpar, w=[Bpar])
                k.dma("sp", abc[:], alog_i.partition_broadcast(128), Bpar, w=[Bpar])
                k.dma("sp", dsk[:], dsk_i.partition_broadcast(128), Bpar, w=[Bpar])
                k.op("act", lambda e: e.activation(out=abc[:], in_=abc[:], func=AF.Exp), r=[Bpar], w=[Bpar])
                k.op("dve", lambda e: e.tensor_scalar(out=abc[:], in0=abc[:], scalar1=-1.0, scalar2=None, op0=ALU.mult), r=[Bpar], w=[Bpar])

                xs0 = sb("xs0", [128, 64, 64], BF16, s3)
                xs = [xs0, xs0]
                xsB0 = Buf("xs0")
                xsB = [xsB0, xsB0]
                xdt0 = sb("xdt0", [128, 64, 64], BF16, s3)
                xdt = [xdt0, xdt0]
                xdtB0 = Buf("xdt0")
                xdtB = [xdtB0, xdtB0]
                Bt = [sb("Bt%d" % i, [128, 8, 128], BF16, s3) for i in range(2)]
                BtB = [Buf("Bt%d" % i) for i in range(2)]
                BCT = [sb("BCT%d" % i, [128, 16, 128], BF16, s3) for i in range(2)]
                BCTB = [Buf("BCT%d" % i) for i in range(2)]
                dtr = [sb("dtr%d" % i, [128, 128], F32, s3) for i in range(2)]
                dtrB = [Buf("dtr%d" % i) for i in range(2)]
                dtv = [sb("dtv%d" % i, [128, 64], F32, s3) for i in range(2)]
                dav = [sb("dav%d" % i, [128, 64], F32, s3) for i in range(2)]
                tmp1 = [sb("tmpa%d" % i, [128, 64], F32, s3) for i in range(2)]
                tmp2 = [sb("tmpb%d" % i, [128, 64], F32, s3) for i in range(2)]
                dvB = [Buf("dv%d" % i) for i in range(2)]
                eacs = [sb("eacs%d" % i, [128, 64], F32, s3) for i in range(2)]
                din_ = [sb("din%d" % i, [128, 64], F32, s3) for i in range(2)]
                elast = [sb("elast%d" % i, [128, 64], F32, s3) for i in range(2)]
                stB = [Buf("stt%d" % i) for i in range(2)]
                S = sb("S", [128, 8, 512], F32, s3)
                Sb_ = sb("Sb", [128, 8, 512], BF16, s3)
                SB = [Buf("S%d" % g) for g in range(8)]
                lseg = [sb("lseg%d" % i, [128, 8, 128], F32, s3) for i in range(3)]
                lsegB = [Buf("lseg%d" % i) for i in range(3)]
                LT = [sb("LT%d" % i, [128, 8, 128], BF16, s3) for i in range(3)]
                LTB = [Buf("LT%d" % i) for i in range(3)]
                CBm = [sb("CBm%d" % i, [128, 128], BF16, s3) for i in range(3)]
                CBmB = [Buf("CBm%d" % i) for i in range(3)]
                MT = [sb("MT%d" % i, [128, 8, 128], BF16, s3) for i in range(3)]
                MTB = [Buf("MT%d" % i) for i in range(3)]
                xdd = [sb("xdd%d" % i, [128, 8, 64], BF16, s3) for i in range(2)]
                xddB = [Buf("xdd%d" % i) for i in range(2)]
                t1 = [sb("t1%d" % i, [128, 8, 64], F32, s3) for i in range(2)]
                t1B = [Buf("t1%d" % i) for i in range(2)]
                yo = [sb("yo%d" % i, [128, 8, 64], F32, s3) for i in range(2)]
                yoB = [Buf("yo%d" % i) for i in range(2)]
                yfl = [sb("yfl%d" % i, [128, 512], F32, s3) for i in range(2)]
                yflB = [Buf("yfl%d" % i) for i in range(2)]
                szl = [sb("szl%d" % i, [128, 512], BF16, s3) for i in range(2)]
                szlB = [Buf("szl%d" % i) for i in range(2)]
                yg = [sb("yg%d" % i, [128, 512], F32, s3) for i in range(2)]
                ygB = [Buf("yg%d" % i) for i in range(2)]
                ss8 = [sb("ss8%d" % i, [128, 16], F32, s3) for i in range(2)]
                ss8B = [Buf("ss8%d" % i) for i in range(2)]
                jk3 = sb("jk3", [128, 512], BF16, s3)
                Bjk3 = Buf("jk3")
                yob = [sb("yob%d" % i, [128, 512], BF16, s3) for i in range(2)]
                yobB = [Buf("yob%d" % i) for i in range(2)]

                cnt = {"g": 0}

                def xdt_chunk(ci):
                    k.op("dve", lambda e: e.tensor_tensor(out=xdt[ci][:], in0=xs[ci][:], in1=bc(dtv[ci][:].unsqueeze(2), [128, 64, 64]), op=ALU.mult), r=[xsB[ci], dvB[ci]], w=xdtGG[ci])

                def conv_chunk(c, fwd, ci):
                    p_ = pre[ci]
                    pB_ = preB[ci]
                    for g in range(8):
                        ps, pB = PS.one()
                        k.op("pe", lambda e: e.matmul(ps[:, :], lhsT=ones1[:, :], rhs=cb2[:, g * 512:(g + 1) * 512], start=True, stop=False),
                             r=[Bcb], w=[pB], sig=False)
                        for q in range(4):
                            ct = 4 * g + q
                            for jj in range(4):
                                k.op("pe", lambda e: e.matmul(ps[:, q * 128:(q + 1) * 128], lhsT=p_[:, ct, jj:jj + 128], rhs=dg[:, ct, jj, :], start=False, stop=(jj == 3)),
                                     r=[pB_, Bdg], w=[pB], sig=(q == 3 and jj == 3))
                        k.op("act", lambda e: e.activation(out=xs[ci][:, 8 * g:8 * g + 8, :].rearrange("p h d -> p (h d)"), in_=ps[:, :], func=AF.Silu), r=[pB], w=[xsB[ci]])
                    for hb in range(2):
                        ps, pB = PS.one()
                        k.op("pe", lambda e: e.matmul(ps[:, :], lhsT=ones1[:, :], rhs=cb2[:, 4096 + hb * 512:4096 + (hb + 1) * 512], start=True, stop=False), r=[Bcb], w=[pB], sig=False)
                        for q in range(4):
                            ct = 32 + 4 * hb + q
                            for jj in range(4):
                                k.op("pe", lambda e: e.matmul(ps[:, q * 128:(q + 1) * 128], lhsT=p_[:, ct, jj:jj + 128], rhs=dg[:, ct, jj, :], start=False, stop=(jj == 3)),
                                     r=[pB_, Bdg], w=[pB], sig=(q == 3 and jj == 3))
                        k.op("act", lambda e: e.activation(out=Bt[ci][:, 4 * hb:4 * hb + 4, :].rearrange("p g n -> p (g n)"), in_=ps[:, :], func=AF.Silu), r=[pB], w=[BtB[ci]])
                    for hb in range(4):
                        ps, pB = PS.one()
                        for q in range(4):
                            ct = 32 + 4 * hb + q
                            k.op("pe", lambda e: e.matmul(ps[:, q * 128:(q + 1) * 128], lhsT=cb2[:, ct * 128:(ct + 1) * 128], rhs=ones1[:, :], start=True, stop=False), r=[Bcb], w=[pB], sig=False)
                            for jj in range(4):
                                k.op("pe", lambda e: e.matmul(ps[:, q * 128:(q + 1) * 128], lhsT=dg[:, ct, jj, :], rhs=p_[:, ct, jj:jj + 128], start=False, stop=(jj == 3)),
                                     r=[pB_, Bdg], w=[pB], sig=(q == 3 and jj == 3))
                        k.op("act", lambda e: e.activation(out=BCT[ci][:, 4 * hb:4 * hb + 4, :].rearrange("p g n -> p (g n)"), in_=ps[:, :], func=AF.Silu), r=[pB], w=[BCTB[ci]])

                def dt_chunk(c, fwd, ci):
                    off = 0 if fwd else 64
                    x_ = tmp1[ci]
                    B_ = dvB[ci]
                    k.op("dve", lambda e: e.tensor_tensor(out=x_[:], in0=dtr[ci][:, off:off + 64], in1=dtb[:, off:off + 64], op=ALU.add), r=[dtrB[ci], Bpar], w=[B_])
                    k.op("dve", lambda e: e.tensor_scalar(out=dtv[ci][:], in0=x_[:], scalar1=0.0, scalar2=None, op0=ALU.max), r=[B_], w=[B_])
                    k.op("dve", lambda e: e.tensor_scalar(out=tmp2[ci][:], in0=x_[:], scalar1=0.0, scalar2=None, op0=ALU.min), r=[B_], w=[B_])
                    k.op("dve", lambda e: e.tensor_tensor(out=tmp2[ci][:], in0=tmp2[ci][:], in1=dtv[ci][:], op=ALU.subtract), r=[B_], w=[B_])
                    k.op("act", lambda e: e.activation(out=tmp2[ci][:], in_=tmp2[ci][:], func=AF.Exp), r=[B_], w=[B_])
                    k.op("act", lambda e: e.activation(out=tmp2[ci][:], in_=tmp2[ci][:], func=AF.Ln, bias=one_ap, scale=1.0), r=[B_, Bc], w=[B_])
                    k.op("dve", lambda e: e.tensor_tensor(out=dtv[ci][:], in0=dtv[ci][:], in1=tmp2[ci][:], op=ALU.add), r=[B_], w=[B_])
                    if c == 0:
                        k.op("dve", lambda e: e.tensor_tensor(out=dtv[ci][:], in0=dtv[ci][:], in1=bc(cst[:, 2, 112:113], [128, 64]), op=ALU.mult), r=[Bc, B_], w=[B_])
                    k.op("dve", lambda e: e.tensor_tensor(out=dav[ci][:], in0=dtv[ci][:], in1=abc[:, off:off + 64], op=ALU.mult), r=[B_, Bpar], w=[B_])
                    ps, pB = PS.one()
                    k.op("pe", lambda e: e.matmul(ps[:, 0:64], lhsT=(TRI_LE if fwd else TRI_GE), rhs=dav[ci][:], start=True, stop=True), r=[B_, Bc], w=[pB], sig=False)
                    k.op("pe", lambda e: e.matmul(ps[:, 64:128], lhsT=(SGT if fwd else SLT), rhs=dav[ci][:], start=True, stop=True), r=[B_, Bc], w=[pB], sig=False)
                    k.op("pe", lambda e: e.matmul(ps[:, 128:192], lhsT=ONESF, rhs=dav[ci][:], start=True, stop=True), r=[B_, Bc], w=[pB])
                    k.op("act", lambda e: e.activation(out=eacs[ci][:], in_=ps[:, 0:64], func=AF.Exp), r=[pB], w=[stB[ci]])
                    k.op("act", lambda e: e.activation(out=din_[ci][:], in_=ps[:, 64:128], func=AF.Exp), r=[pB], w=[stB[ci]])
                    k.op("act", lambda e: e.activation(out=elast[ci][:], in_=ps[:, 128:192], func=AF.Exp), r=[pB], w=[stB[ci]])

                xdtGG = [[Buf("xdtG%d" % g) for g in range(8)]]
                xdtGG.append(xdtGG[0])
                tD = [sb("tD%d" % i, [128, 8, 64], BF16, s3) for i in range(3)]
                tDB = [Buf("tD%d" % i) for i in range(3)]

                def stageA(c, fwd, ci, g, first, part=0):
                    gi_ = g % 3
                    hs = slice(8 * g, 8 * g + 8)
                    if part in (0, 1):
                        stageA1(c, fwd, ci, g, gi_, hs)
                    if part in (0, 2):
                        stageA2(c, fwd, ci, g, gi_, hs)

                def stageA1(c, fwd, ci, g, gi_, hs):
                    k.op("dve", lambda e: e.tensor_tensor(out=lseg[gi_][:], in0=bc((TRI_LE if fwd else TRI_GE).unsqueeze(1), [128, 8, 128]),
                                                           in1=bc(dav[ci][:, hs].unsqueeze(2), [128, 8, 128]), op=ALU.mult), r=[dvB[ci], Bc], w=[lsegB[gi_]])
                    psg, pBa, pBb = PS.two()
                    for h2 in range(2):
                        k.op("pe", lambda e: e.matmul(psg[:, h2 * 512:(h2 + 1) * 512], lhsT=(SGT if fwd else SLT), rhs=lseg[gi_][:, 4 * h2:4 * h2 + 4, :].rearrange("p h l -> p (h l)"), start=True, stop=True),
                             r=[lsegB[gi_], Bc], w=[pBa, pBb], sig=(h2 == 1))
                    k.op("act", lambda e: e.activation(out=LT[gi_][:].rearrange("p h l -> p (h l)"), in_=psg[:, :], func=AF.Exp), r=[pBa, pBb], w=[LTB[gi_]])
                    if fwd:
                        k.op("pool", lambda e: e.tensor_tensor(out=tD[gi_][:], in0=xs[ci][:, hs, :], in1=bc(dsk[:, hs].unsqueeze(2), [128, 8, 64]), op=ALU.mult), r=[xsB[ci], Bpar], w=[tDB[gi_]])

                def stageA2(c, fwd, ci, g, gi_, hs):
                    pc, pcB = PS.one()
                    k.op("pe", lambda e: e.matmul(pc[:, 0:128], lhsT=BCT[ci][:, g, :], rhs=BCT[ci][:, 8 + g, :], start=True, stop=True), r=[BCTB[ci]], w=[pcB])
                    k.op("dve", lambda e: e.tensor_tensor(out=CBm[gi_][:], in0=pc[:, 0:128], in1=(MASKF if fwd else MASKB), op=ALU.mult), r=[pcB, Bc], w=[CBmB[gi_]])
                    k.op("dve", lambda e: e.tensor_tensor(out=MT[gi_][:], in0=LT[gi_][:], in1=bc(CBm[gi_][:].unsqueeze(1), [128, 8, 128]), op=ALU.mult), r=[LTB[gi_], CBmB[gi_]], w=[MTB[gi_]])

                def mk_xdt(ci, g):
                    hs = slice(8 * g, 8 * g + 8)
                    k.op("dve", lambda e: e.tensor_tensor(out=xdt[ci][:, hs, :], in0=xs[ci][:, hs, :], in1=bc(dtv[ci][:, hs].unsqueeze(2), [128, 8, 64]), op=ALU.mult), r=[xsB[ci], dvB[ci]], w=[xdtGG[ci][g]])

                def stageB(c, fwd, ci, g, need_y, first, m):
                    gi_ = g % 2
                    hs = slice(8 * g, 8 * g + 8)
                    if need_y:
                        py, pyB = PS.one()
                        if fwd:
                            k.op("pe", lambda e: e.matmul(py[:, :], lhsT=ident, rhs=tD[g % 3][:].rearrange("p h d -> p (h d)"), start=True, stop=False), r=[tDB[g % 3], Bc], w=[pyB], sig=False)
                        for h in range(8):
                            k.op("pe", lambda e: e.matmul(py[:, h * 64:(h + 1) * 64], lhsT=MT[g % 3][:, h, :], rhs=xdt[ci][:, 8 * g + h, :], start=(not fwd), stop=((not fwd) or h == 7)),
                                 r=[MTB[g % 3], xdtGG[ci][g]], w=[pyB], sig=(h == 7))
                        if not first:
                            pz, pzB = PS.one()
                            k.op("pe", lambda e: e.matmul(pz[:, :], lhsT=BCT[ci][:, 8 + g, :], rhs=Sb_[:, g, :], start=True, stop=True), r=[BCTB[ci], SB[g]], w=[pzB])
                    k.op("dve", lambda e: e.tensor_tensor(out=xdd[gi_][:], in0=xdt[ci][:, hs, :], in1=bc(din_[ci][:, hs].unsqueeze(2), [128, 8, 64]), op=ALU.mult), r=[xdtGG[ci][g], stB[ci]], w=[xddB[gi_]])
                    pi, piB = PS.one()
                    k.op("pe", lambda e: e.matmul(pi[:, :], lhsT=Bt[ci][:, g, :], rhs=xdd[gi_][:].rearrange("p h d -> p (h d)"), start=True, stop=True), r=[BtB[ci], xddB[gi_]], w=[piB])
                    if need_y:
                        if not first:
                            k.op("dve", lambda e: e.tensor_tensor(out=t1[gi_][:], in0=pz[:, :].rearrange("p (h d) -> p h d", d=64), in1=bc(eacs[ci][:, hs].unsqueeze(2), [128, 8, 64]), op=ALU.mult),
                                 r=[pzB, stB[ci]], w=[t1B[gi_]])
                            k.op("dve", lambda e: e.tensor_tensor(out=yo[gi_][:], in0=py[:, :].rearrange("p (h d) -> p h d", d=64), in1=t1[gi_][:], op=ALU.add), r=[pyB, t1B[gi_]], w=[yoB[gi_]])
                        else:
                            k.op("dve", lambda e: e.tensor_copy(out=yo[gi_][:], in_=py[:, :].rearrange("p (h d) -> p h d", d=64)), r=[pyB], w=[yoB[gi_]])
                        r0 = c * 128
                        if fwd:
                            k.dma("sp", yfS[r0:r0 + 128, g * 512:(g + 1) * 512], yo[gi_][:].rearrange("p h d -> p (h d)"), yoB[gi_], r=[yoB[gi_]], w=[B_yf])
                        else:
                            bg_st.need(m)
                            m2 = m % 2
                            k.op("dve", lambda e: e.tensor_tensor(out=yo[gi_][:].rearrange("p h d -> p (h d)"), in0=yo[gi_][:].rearrange("p h d -> p (h d)"), in1=yfl[m2][:], op=ALU.add), r=[yoB[gi_], yflB[m2]], w=[yoB[gi_]])
                            k.op("pool", lambda e: e.tensor_tensor(out=yg[m2][:], in0=yo[gi_][:].rearrange("p h d -> p (h d)"), in1=szl[m2][:], op=ALU.mult),
                                 r=[yoB[gi_], szlB[m2]], w=[ygB[m2]])
                            k.op("dve", lambda e: e.memset(ss8[m2][:, 0:1], 0.0), w=[ss8B[m2]])
                            k.op("act", lambda e: e.activation(out=jk3[:], in_=yg[m2][:], func=AF.Square, accum_out=ss8[m2][:, 0:1]), r=[ygB[m2], ss8B[m2]], w=[Bjk3, ss8B[m2]])
                            k.op("act", lambda e: e.activation(out=ss8[m2][:, 1:2], in_=ss8[m2][:, 0:1], func=AF.Ln, bias=eps_ap, scale=1.0 / 512), r=[ss8B[m2], Bc], w=[ss8B[m2]])
                            k.op("act", lambda e: e.activation(out=ss8[m2][:, 2:3], in_=ss8[m2][:, 1:2], func=AF.Exp, scale=-0.5), r=[ss8B[m2]], w=[ss8B[m2]])
                            k.op("dve", lambda e: e.scalar_tensor_tensor(out=yob[m2][:], in0=yg[m2][:], scalar=ss8[m2][:, 2:3], in1=nwl[m2][:], op0=ALU.mult, op1=ALU.mult),
                                 r=[ygB[m2], ss8B[m2], nwlB[m2]], w=[yobB[m2]])
                            k.dma("sp", mixS[r0:r0 + 128, 2048 + g * 512:2048 + (g + 1) * 512], yob[m2][:], yobB[m2], r=[yobB[m2]], w=[B_mix])
                    Sg = S[:, g, :].rearrange("p (h d) -> p h d", d=64)
                    if first:
                        k.op("dve", lambda e: e.tensor_copy(out=S[:, g, :], in_=pi[:, :]), r=[piB], w=[SB[g]])
                    else:
                        for h in range(8):
                            k.op("dve", lambda e: e.scalar_tensor_tensor(out=S[:, g, h * 64:(h + 1) * 64], in0=S[:, g, h * 64:(h + 1) * 64], scalar=elast[ci][:, 8 * g + h:8 * g + h + 1],
                                                                         in1=pi[:, h * 64:(h + 1) * 64], op0=ALU.mult, op1=ALU.add), r=[SB[g], stB[ci], piB], w=[SB[g]], noself=(h > 0))
                    k.op("act", lambda e: e.activation(out=Sb_[:, g, :], in_=S[:, g, :], func=AF.Copy), r=[SB[g]], w=[SB[g]])

                B_stash = Buf("stash", multi=True)
                s3f = ExitStack()
                s3f.__enter__()
                cw = sb("cw", [128, 4, 48], F32, s3f)
                Bcw = Buf("cw")
                k.dma("sp", cw[:].rearrange("p a b -> p (a b)"), cw_i, Bcw, w=[Bcw])
                dg = sb("dg", [128, 48, 4, 128], BF16, s3f)
                Bdg = Buf("dg")
                for ct in range(48):
                    for jj in range(4):
                        k.op("pool" if (ct % 2) else "dve", lambda e: e.tensor_scalar(out=dg[:, ct, jj, :], in0=cst[:, 0, :], scalar1=cw[:, jj, ct:ct + 1], scalar2=None, op0=ALU.mult),
                             r=[Bcw, Bc], w=[Bdg])
                cbf = sb("cbf", [48, 128], F32, s3f)
                cbh = sb("cbh", [48, 2, 128], BF16, s3f)
                cbt = sb("cbt", [48, 128], F32, s3f)
                Bcb0 = Buf("cb0")
                Bcb = Buf("cb")
                k.dma("sp", cbf[:], cb_i.rearrange("o (c p) -> (o c) p", p=128), Bcb0, w=[Bcb0])
                k.op("dve", lambda e: e.tensor_copy(out=cbh[:, 0, :], in_=cbf[:]), r=[Bcb0], w=[Bcb0])
                k.op("dve", lambda e: e.tensor_copy(out=cbt[:], in_=cbh[:, 0, :]), r=[Bcb0], w=[Bcb0])
                k.op("dve", lambda e: e.tensor_tensor(out=cbt[:], in0=cbf[:], in1=cbt[:], op=ALU.subtract), r=[Bcb0], w=[Bcb0])
                k.op("dve", lambda e: e.tensor_copy(out=cbh[:, 1, :], in_=cbt[:]), r=[Bcb0], w=[Bcb0])
                B_cbD = Buf("cbD", multi=True)
                k.dma("sp", cbD.rearrange("a (c p) -> c a p", p=128), cbh[:], Bcb0, r=[Bcb0], w=[B_cbD])
                cb2 = sb("cb2", [2, 6144], BF16, s3f)
                k.dma("sp", cb2[:], cbD, Bcb, r=[B_cbD], w=[Bcb])
                ones1 = sb("ones1", [2, 128], BF16, s3f)
                k.op("dve", lambda e: e.memset(ones1[:], 1.0), w=[Bcb])
                pre = [sb("pre%d" % i, [128, 48, 131], BF16, s3f) for i in range(2)]
                preB = [Buf("pre%d" % i) for i in range(2)]

                def mk_ckf(n):
                    return lambda: (k.dma("sp", pre[n % 2][:], xbcT[:, :, n * 128:n * 128 + 131].rearrange("c p t -> p c t"), preB[n % 2], r=[B_xbc], w=[preB[n % 2]]),
                                    k.dma("sp", dtr[n % 2][:], dtS[n * 128:(n + 1) * 128, :], dtrB[n % 2], r=[B_dt], w=[dtrB[n % 2]]))
                ckf_st = Stream([mk_ckf(n) for n in range(NCHP)], 1)

                for c in range(NCHP):
                    ci = c % 2
                    ckf_st.need(c)
                    first = (c == 0)
                    need_y = (c > 0)
                    conv_chunk(c, True, ci)
                    if c > 0:
                        r0 = c * 128
                        k.dma("sp", xsS[r0:r0 + 128, :], xs[ci][:].rearrange("p h d -> p (h d)"), xsB[ci], r=[xsB[ci]], w=[B_stash])
                        k.dma("sp", BtS[r0:r0 + 128, :], Bt[ci][:].rearrange("p g n -> p (g n)"), BtB[ci], r=[BtB[ci]], w=[B_stash])
                        k.dma("sp", BCTS[c], BCT[ci][:].rearrange("p g n -> p (g n)"), BCTB[ci], r=[BCTB[ci]], w=[B_stash])
                    dt_chunk(c, True, ci)
                    xdt_chunk(ci)
                    if need_y:
                        stageA(c, True, ci, 0, first)
                        stageA(c, True, ci, 1, first)
                    for g in range(8):
                        if need_y and g + 2 < 8:
                            stageA(c, True, ci, g + 2, first, 1)
                        stageB(c, True, ci, g, need_y, first, 0)
                        if need_y and g + 2 < 8:
                            stageA(c, True, ci, g + 2, first, 2)
                s3f.close()
                k.barrier()

                xs[1] = sb("xs_b", [128, 64, 64], BF16, s3)
                xsB[1] = Buf("xs_b")
                xdt[1] = sb("xdt_b", [128, 64, 64], BF16, s3)
                xdtGG[1] = [Buf("xdtGb%d" % g) for g in range(8)]
                nb_n = NCH

                def mk_ckb(nb):
                    c = NCHP - 1 - nb
                    bi = nb % 2
                    r0 = c * 128
                    return lambda: (k.dma("sp", xs[bi][:].rearrange("p h d -> p (h d)"), xsS[r0:r0 + 128, :], xsB[bi], r=[B_stash], w=[xsB[bi]]),
                                    k.dma("sp", Bt[bi][:].rearrange("p g n -> p (g n)"), BtS[r0:r0 + 128, :], BtB[bi], r=[B_stash], w=[BtB[bi]]),
                                    k.dma("sp", BCT[bi][:].rearrange("p g n -> p (g n)"), BCTS[c], BCTB[bi], r=[B_stash], w=[BCTB[bi]]),
                                    k.dma("sp", dtr[bi][:], dtS[r0:r0 + 128, :], dtrB[bi], r=[B_dt], w=[dtrB[bi]]))
                ckb = [mk_ckb(nb) for nb in range(nb_n)]

                def prologue(nb):
                    c = NCHP - 1 - nb
                    dt_chunk(c, False, nb % 2)
                    xdt_chunk(nb % 2)
                def mk_bg(m):
                    c = NCHP - 1 - (m // 8)
                    g = m % 8
                    r0 = c * 128
                    m2 = m % 2
                    return lambda: (k.dma("sp", yfl[m2][:], yfS[r0:r0 + 128, g * 512:(g + 1) * 512], yflB[m2], r=[B_yf], w=[yflB[m2]]),
                                    k.dma("sp", szl[m2][:], szS[r0:r0 + 128, g * 512:(g + 1) * 512], szlB[m2], r=[B_sz], w=[szlB[m2]]),
                                    k.dma("sp", nwl[m2][:], nw_i[:, g * 512:(g + 1) * 512].partition_broadcast(128), nwlB[m2], w=[nwlB[m2]]))
                bg_st = Stream([mk_bg(m) for m in range(NCH * 8)], 1)
                ckb[0]()
                prologue(0)
                for nb in range(nb_n):
                    c = NCHP - 1 - nb
                    ci = nb % 2
                    first = (nb == 0)
                    if nb + 1 < nb_n:
                        ckb[nb + 1]()
                    stageA(c, False, ci, 0, first)
                    stageA(c, False, ci, 1, first)
                    for g in range(8):
                        if g + 2 < 8:
                            stageA(c, False, ci, g + 2, first, 1)
                        if g == 3 and nb + 1 < nb_n:
                            prologue(nb + 1)
                        stageB(c, False, ci, g, True, first, nb * 8 + g)
                        if g + 2 < 8:
                            stageA(c, False, ci, g + 2, first, 2)

            k.barrier()
            with ExitStack() as s4:
                mixl = [sb("mixl%d" % i, [128, 6144], BF16, s4) for i in range(2)]
                mixlB = [Buf("mixl%d" % i) for i in range(2)]
                big = sb("big", [128, 48, 512], BF16, s4)
                Bbig = Buf("big")
                h1 = sb("h1", [128, 4, D], F32, s4)
                h1B = [Buf("h1_%d" % i) for i in range(4)]
                fn = [sb("fn%d" % i, [128, D], BF16, s4) for i in range(2)]
                fnB = [Buf("fn%d" % i) for i in range(2)]
                fT = sb("fT", [128, 16, 512], BF16, s4)
                BfT = Buf("fT")
                wo = [sb("wo%d" % i, [128, 8, 512], BF16, s4) for i in range(2)]
                woB = [Buf("wo%d" % i) for i in range(2)]
                wg = [sb("wg%d" % i, [128, 2, 16, 128], BF16, s4) for i in range(2)]
                wgB = [Buf("wg%d" % i) for i in range(2)]
                wd = [sb("wd%d" % i, [128, 4, 512], BF16, s4) for i in range(3)]
                wdB = [Buf("wd%d" % i) for i in range(3)]
                sg = [sb("sg%d" % i, [128, 512], F32, s4) for i in range(2)]
                sgB = [Buf("sg%d" % i) for i in range(2)]
                st4 = sb("st4", [128, 16], F32, s4)
                st4B = [Buf("st4_%d" % i) for i in range(4)]
                jk4 = sb("jk4", [128, D], BF16, s4)
                Bjk4 = Buf("jk4")
                ntile4 = (NCH + 3) // 4

                def mk_wo(i):
                    return lambda: k.dma("sp", wo[i % 2][:].rearrange("p a b -> p (a b)"), wb_out[i % 24], woB[i % 2], r=[B_wb["out"]], w=[woB[i % 2]])
                wo_st = Stream([mk_wo(i) for i in range(ntile4 * 24)], 1)

                def mk_wg(i):
                    return lambda: k.dma("sp", wg[i % 2][:].rearrange("p a b c -> p (a b c)"), wb_gu[i % 44], wgB[i % 2], r=[B_wb["gu"]], w=[wgB[i % 2]])
                wg_st = Stream([mk_wg(i) for i in range(ntile4 * 44)], 1)

                def mk_wd(i):
                    return lambda: k.dma("sp", wd[i % 3][:].rearrange("p a b -> p (a b)"), wb_dn[i % 44], wdB[i % 3], r=[B_wb["dn"]], w=[wdB[i % 3]])
                wd_st = Stream([mk_wd(i) for i in range(ntile4 * 44)], 2)
                def mk_ml(i):
                    return lambda: k.dma("sp", mixl[i % 2][:], mixS[(1 + i) * 128:(2 + i) * 128, :], mixlB[i % 2], r=[B_mix], w=[mixlB[i % 2]])
                ml_st = Stream([mk_ml(i) for i in range(NCH)], 1)
                for j in range(ntile4):
                    c0 = 1 + 4 * j
                    ncj = min(4, NCHP - c0)
                    TW = ncj * 128
                    wo_st.need(j * 24 - 1)
                    for ci in range(ncj):
                        gci = 4 * j + ci
                        ml_st.need(gci)
                        ml = mixl[gci % 2]
                        mB = mixlB[gci % 2]
                        for o8 in range(6):
                            pt, pB = PS.one()
                            ptb = pt.bitcast(BF16).rearrange("p (a b) -> p a b", b=128)
                            for q8 in range(8):
                                kc = o8 * 8 + q8
                                k.op("pe", lambda e: e.transpose(out=ptb[:, q8, :], in_=ml[:, kc * 128:(kc + 1) * 128], identity=ident), r=[mB, Bc], w=[pB], sig=(q8 == 7))
                            k.op("act" if o8 % 2 else "dve", (lambda e: e.activation(out=big[:, o8 * 8:o8 * 8 + 8, ci * 128:(ci + 1) * 128], in_=ptb[:, 0:8, :], func=AF.Copy)) if o8 % 2 else
                                 (lambda e: e.tensor_copy(out=big[:, o8 * 8:o8 * 8 + 8, ci * 128:(ci + 1) * 128], in_=ptb[:, 0:8, :])), r=[pB], w=[Bbig])
                    for ci in range(ncj):
                        r0 = (c0 + ci) * 128
                        k.dma("sp", h1[:, ci, :], xin[r0:r0 + 128, :], h1B[ci], w=[h1B[ci]])
                    for cg in range(4):
                        pss = [PS.one() for _ in range(ncj)]
                        for sub in range(6):
                            io = (j * 4 + cg) * 6 + sub
                            wo_st.need(io)
                            w_ = wo[io % 2]
                            wB = woB[io % 2]
                            for ci in range(ncj):
                                ps, pB = pss[ci]
                                for q8 in range(8):
                                    kc = sub * 8 + q8
                                    k.op("pe", lambda e: e.matmul(ps[:, :], lhsT=big[:, kc, ci * 128:(ci + 1) * 128], rhs=w_[:, q8, :], start=(kc == 0), stop=(kc == 47)),
                                         r=[Bbig, wB], w=[pB], sig=(q8 == 7))
                        for ci in range(ncj):
                            ps, pB = pss[ci]
                            k.op("dve", lambda e: e.tensor_tensor(out=h1[:, ci, cg * 512:(cg + 1) * 512], in0=ps[:, :], in1=h1[:, ci, cg * 512:(cg + 1) * 512], op=ALU.add),
                                 r=[pB, h1B[ci]], w=[h1B[ci]])
                    wg_st.need(j * 44 - 1)
                    for ci in range(ncj):
                        so = 4 * ci
                        sB = st4B[ci]
                        k.op("dve", lambda e: e.memset(st4[:, so:so + 1], 0.0), w=[sB])
                        k.op("act", lambda e: e.activation(out=jk4[:], in_=h1[:, ci, :], func=AF.Square, accum_out=st4[:, so:so + 1]), r=[h1B[ci], sB], w=[Bjk4, sB])
                        k.op("act", lambda e: e.activation(out=st4[:, so + 1:so + 2], in_=st4[:, so:so + 1], func=AF.Sqrt, bias=eps_ap, scale=1.0 / D), r=[sB, Bc], w=[sB])
                        k.op("dve", lambda e: e.reciprocal(out=st4[:, so + 2:so + 3], in_=st4[:, so + 1:so + 2]), r=[sB], w=[sB])
                        f_ = fn[ci % 2]
                        fB = fnB[ci % 2]
                        k.op("act", lambda e: e.activation(out=f_[:], in_=h1[:, ci, :], func=AF.Copy, scale=st4[:, so + 2:so + 3]), r=[h1B[ci], sB], w=[fB])
                        for half in range(2):
                            pt, pB = PS.one()
                            ptb = pt.bitcast(BF16).rearrange("p (a b) -> p a b", b=128)
                            for q8 in range(8):
                                kc = half * 8 + q8
                                k.op("pe", lambda e: e.transpose(out=ptb[:, q8, :], in_=f_[:, kc * 128:(kc + 1) * 128], identity=ident), r=[fB, Bc], w=[pB], sig=(q8 == 7))
                            k.op("dve", lambda e: e.tensor_tensor(out=fT[:, half * 8:half * 8 + 8, ci * 128:(ci + 1) * 128], in0=ptb[:, 0:8, :],
                                                                  in1=bc(gffn[:, half * 8:half * 8 + 8].unsqueeze(2), [128, 8, 128]), op=ALU.mult), r=[pB, Bc], w=[BfT])
                    for ht in range(44):
                        ig = j * 44 + ht
                        wg_st.need(ig)
                        w_ = wg[ig % 2]
                        wB = wgB[ig % 2]
                        if ht == 40:
                            wd_st.need(j * 44 - 1)
                        pg, pgB = PS.one()
                        pu, puB = PS.one()
                        for kc in range(16):
                            k.op("pe", lambda e: e.matmul(pg[:, :TW], lhsT=w_[:, 0, kc, :], rhs=fT[:, kc, :TW], start=(kc == 0), stop=(kc == 15)), r=[wB, BfT], w=[pgB], sig=(kc == 15))
                        for kc in range(16):
                            k.op("pe", lambda e: e.matmul(pu[:, :TW], lhsT=w_[:, 1, kc, :], rhs=fT[:, kc, :TW], start=(kc == 0), stop=(kc == 15)), r=[wB, BfT], w=[puB], sig=(kc == 15))
                        s_ = sg[ht % 2]
                        sB = sgB[ht % 2]
                        k.op("act", lambda e: e.activation(out=s_[:, :TW], in_=pg[:, :TW], func=AF.Silu), r=[pgB], w=[sB])
                        k.op("dve", lambda e: e.tensor_tensor(out=big[:, ht, :TW], in0=s_[:, :TW], in1=pu[:, :TW], op=ALU.mult), r=[sB, puB], w=[Bbig])
                    for cg in range(4):
                        pss = [PS.one() for _ in range(ncj)]
                        for sub in range(11):
                            idn = (j * 4 + cg) * 11 + sub
                            wd_st.need(idn)
                            if cg == 2 and sub == 0:
                                ml_st.need(4 * (j + 1) - 1)
                            w_ = wd[idn % 3]
                            wB = wdB[idn % 3]
                            for ci in range(ncj):
                                ps, pB = pss[ci]
                                for q4 in range(4):
                                    kc = sub * 4 + q4
                                    k.op("pe", lambda e: e.matmul(ps[:, :], lhsT=big[:, kc, ci * 128:(ci + 1) * 128], rhs=w_[:, q4, :], start=(kc == 0), stop=(kc == 43)),
                                         r=[Bbig, wB], w=[pB], sig=(q4 == 3))
                        for ci in range(ncj):
                            ps, pB = pss[ci]
                            k.op("dve", lambda e: e.tensor_tensor(out=h1[:, ci, cg * 512:(cg + 1) * 512], in0=ps[:, :], in1=h1[:, ci, cg * 512:(cg + 1) * 512], op=ALU.add),
                                 r=[pB, h1B[ci]], w=[h1B[ci]])
                    for ci in range(ncj):
                        r0 = (c0 - 1 + ci) * 128
                        k.dma("sp", yout[r0:r0 + 128, :], h1[:, ci, :], h1B[ci], r=[h1B[ci]], w=[B_y])
            k.final_wait([B_y])
    return nc


def _host_consts():
    c = np.zeros((128, 7, 128), np.float32)
    i = np.arange(128)
    c[:, 0, :] = np.eye(128)
    c[:, 1, :] = (i[:, None] <= i[None, :])
    c[:, 2, :] = (i[:, None] >= i[None, :])
    c[:, 3, :] = (i[:, None] > i[None, :])
    c[:, 4, :] = (i[:, None] < i[None, :])
    c[:, 5, :] = 1.0
    c[:, 6, :] = ((i[:, None] // 64) == (i[None, :] // 64))
    return c.reshape(128, 7 * 128)


def _rbl_tables(rpb, NCH):
    NR = 2 * NCH
    cols = np.arange(64)
    cs = np.clip(cols - 8, 0, 48)
    vars_R = [2, 0, 1, NCH - 2, NCH - 1]
    p = np.arange(128)
    i2, cp = p // 64, p % 64
    q = np.arange(128)
    j2, cq = q // 64, q % 64
    idx_dr = np.zeros((5, 128, 5, 128), np.int64)
    idx_dc = np.zeros((5, 128, 5, 128), np.int64)
    mask = np.zeros((5, 128, 5, 128), np.float32)
    for v, R in enumerate(vars_R):
        KT0 = min(max(R - 2, 0), NCH - 5)
        for kt in range(5):
            krow = 2 * KT0 + 2 * kt + i2[:, None]
            qrow = 2 * R + j2[None, :]
            rs = np.clip(qrow - 4, 0, NR - 8)
            dr = krow - qrow + 7
            okr = (krow >= rs) & (krow < rs + 8)
            dc = cp[:, None] - cq[None, :] + 15
            okc = (cp[:, None] >= cs[cq][None, :]) & (cp[:, None] < cs[cq][None, :] + 16)
            ok = okr & okc
            idx_dr[v, :, kt, :] = np.clip(dr, 0, 14)
            idx_dc[v, :, kt, :] = np.clip(dc, 0, 30)
            mask[v, :, kt, :] = ok
    rb = rpb[:, idx_dr, idx_dc]
    rb = np.ascontiguousarray(np.transpose(rb, (1, 0, 2, 3, 4))).reshape(5, 32, 128, 640)
    return rb.astype(np.float32), mask.reshape(5, 128, 640)


_CACHE = {}


def _prep_shared(NCH, meta_tokens, g_mix, w_in, q_norm, k_norm, rpb, conv_w, conv_b, dt_bias_f, dt_bias_b,
                 a_log_f, a_log_b, d_skip, ssd_norm, w_out, g_ffn, w_gate, w_up, w_down):
    f = np.float32
    w_in = np.asarray(w_in[0], f)
    fm = np.concatenate([w_in[:, 0:2560], w_in[:, 7168:13312]], axis=1)
    w_fm = np.ascontiguousarray(fm.reshape(16, 128, 68, 128).transpose(2, 1, 0, 3)).reshape(68, 128, 2048)
    tm = w_in[:, 2560:7168]
    w_tm = np.ascontiguousarray(tm.reshape(16, 128, 9, 512).transpose(2, 1, 0, 3)).reshape(9, 128, 8192)
    wd_ = w_in[:, 13312:13440]
    w_dt = np.ascontiguousarray(wd_.reshape(16, 128, 128).transpose(1, 0, 2)).reshape(128, 2048)
    wo = np.asarray(w_out[0], f)
    w_o = np.ascontiguousarray(wo.reshape(6, 8, 128, 4, 512).transpose(3, 0, 2, 1, 4)).reshape(24, 128, 4096)
    wg = np.asarray(w_gate[0], f).reshape(16, 128, 44, 128)
    wu = np.asarray(w_up[0], f).reshape(16, 128, 44, 128)
    w_gu = np.ascontiguousarray(np.stack([wg, wu], 0).transpose(3, 2, 0, 1, 4)).reshape(44, 128, 4096)
    wdn = np.asarray(w_down[0], f)
    w_dn = np.ascontiguousarray(wdn.reshape(11, 4, 128, 4, 512).transpose(3, 0, 2, 1, 4)).reshape(44, 128, 2048)
    rb, mk = _rbl_tables(np.asarray(rpb[0], f), NCH)
    cw = np.ascontiguousarray(np.asarray(conv_w[0], f).reshape(4, 48, 128).transpose(2, 0, 1)).reshape(128, 192)
    sh = {
        "w_fm": w_fm, "w_tm": w_tm, "w_dt": w_dt, "w_out": w_o, "w_gu": w_gu, "w_dn": w_dn,
        "gmix": np.ascontiguousarray(np.asarray(g_mix[0], f).reshape(16, 128).T),
        "gffn": np.ascontiguousarray(np.asarray(g_ffn[0], f).reshape(16, 128).T),
        "qkg": np.ascontiguousarray(np.stack([np.tile(np.asarray(q_norm[0], f), 2), np.tile(np.asarray(k_norm[0], f), 2)], 1)),
        "rbl": rb, "msk": mk, "cw": cw,
        "cb": np.asarray(conv_b[0], f).reshape(1, 6144),
        "dtb": np.concatenate([np.asarray(dt_bias_f[0], f), np.asarray(dt_bias_b[0], f)]).reshape(1, 128),
        "alog": np.concatenate([np.asarray(a_log_f[0], f), np.asarray(a_log_b[0], f)]).reshape(1, 128),
        "dsk": np.asarray(d_skip[0], f).reshape(1, 64),
        "nw": np.asarray(ssd_norm[0], f).reshape(1, 4096),
        "cst": _host_consts(),
    }
    return sh


def run_seqs(seqs, NCH, **params):
    meta = np.asarray(params["meta_tokens"], np.float32)
    sh = _prep_shared(NCH, **params)
    if NCH not in _CACHE:
        _CACHE[NCH] = build(NCH)
    nc = _CACHE[NCH]
    in_maps = []
    ncore = 8 if len(seqs) > 1 else 1
    for c in range(ncore):
        s = seqs[c % len(seqs)]
        xin = np.concatenate([np.zeros((112, D), np.float32), meta, np.asarray(s, np.float32)], axis=0)
        m = dict(sh)
        m["xin"] = np.ascontiguousarray(xin)
        in_maps.append(m)
    res = run_bass_kernel_spmd(nc, in_maps, core_ids=list(range(ncore)))
    return [np.asarray(res.results[c]["y"], np.float32) for c in range(len(seqs))]


def kernel(x_prompt, x_sample, meta_tokens, g_mix, w_in, q_norm, k_norm, rpb, conv_w, conv_b,
           dt_bias_f, dt_bias_b, a_log_f, a_log_b, d_skip, ssd_norm, w_out, g_ffn, w_gate, w_up, w_down):
    x_prompt = np.asarray(x_prompt, np.float32)
    x_sample = np.asarray(x_sample, np.float32)
    seqs = [x_prompt[i] for i in range(x_prompt.shape[0])] + [x_sample[i] for i in range(x_sample.shape[0])]
    NCH = seqs[0].shape[0] // 128
    outs = run_seqs(seqs, NCH, meta_tokens=meta_tokens, g_mix=g_mix, w_in=w_in, q_norm=q_norm, k_norm=k_norm, rpb=rpb,
                    conv_w=conv_w, conv_b=conv_b, dt_bias_f=dt_bias_f, dt_bias_b=dt_bias_b, a_log_f=a_log_f,
                    a_log_b=a_log_b, d_skip=d_skip, ssd_norm=ssd_norm, w_out=w_out, g_ffn=g_ffn, w_gate=w_gate,
                    w_up=w_up, w_down=w_down)
    nb = x_prompt.shape[0]
    y_prompt = np.stack(outs[:nb], 0)
    y_sample = np.stack(outs[nb:], 0)
    return (y_prompt, y_sample)
```

```python
from contextlib import ExitStack
import numpy as np
import concourse.bass as bass
import concourse.mybir as mybir
from concourse.bass_utils import run_bass_kernel_spmd

F32 = mybir.dt.float32
BF16 = mybir.dt.bfloat16
AF = mybir.ActivationFunctionType
ALU = mybir.AluOpType

D = 2048
NMETA = 16
GW = 64
DFF = 5632
EPS = 1e-6
SELF_WAIT = True


class Buf:
    def __init__(self, name="", multi=False):
        self.w = {}
        self.r = {}
        self.name = name
        self.multi = multi
        self.dsem = None
        self.dcnt = 0


class K:
    def __init__(self, nc, stack):
        self.nc = nc
        self.stack = stack
        self.E = {"pe": nc.tensor, "act": nc.scalar, "dve": nc.vector, "pool": nc.gpsimd, "sp": nc.sync}
        self.sems = []
        self.esem = {}
        self.ecnt = {}
        for e in ["pe", "act", "dve", "pool"]:
            self.esem[e] = self.new_sem("e_" + e)
            self.ecnt[e] = 0
        self.seen = {e: {} for e in self.E}
        self.smax = {}

    def barrier(self):
        for e in self.E:
            for si, v in self.smax.items():
                if v <= 0 or self.seen[e].get(si, 0) >= v:
                    continue
                if e == "pe" and si == self.esem["pe"]:
                    continue
                self.E[e].wait_ge(self.sems[si], v)
                self.seen[e][si] = v

    def new_sem(self, name):
        h = self.stack.enter_context(self.nc.semaphore(name + "_%d" % len(self.sems)))
        self.sems.append(h)
        return len(self.sems) - 1

    def _waits(self, e, r, w, noself=False):
        need = {}
        for b in r:
            for si, v in b.w.items():
                need[si] = max(need.get(si, 0), v)
        for b in w:
            if not b.multi:
                for si, v in b.w.items():
                    need[si] = max(need.get(si, 0), v)
            for si, v in b.r.items():
                need[si] = max(need.get(si, 0), v)
        for si, v in need.items():
            if e in self.esem and si == self.esem[e]:
                if e == "pe" or not SELF_WAIT or noself:
                    continue
            if self.seen[e].get(si, 0) >= v:
                continue
            self.E[e].wait_ge(self.sems[si], v)
            self.seen[e][si] = v

    def op(self, e, fn, r=(), w=(), sig=True, noself=False):
        self._waits(e, r, w, noself)
        ins = fn(self.E[e])
        si = self.esem[e]
        if sig:
            self.ecnt[e] += 1
            ins.then_inc(self.sems[si], 1)
            tok = self.ecnt[e]
            self.smax[si] = tok
        else:
            assert e == "pe"
            tok = self.ecnt[e] + 1
        for b in r:
            b.r[si] = max(b.r.get(si, 0), tok)
        for b in w:
            if b.multi:
                b.w[si] = max(b.w.get(si, 0), tok)
            else:
                b.w = {si: tok}
                b.r = {}
        return ins

    def dma(self, q, out, in_, slot, r=(), w=()):
        self._waits(q, r, w)
        if slot.dsem is None:
            slot.dsem = self.new_sem("d_" + slot.name)
        ins = self.E[q].dma_start(out=out, in_=in_)
        slot.dcnt += 16
        ins.then_inc(self.sems[slot.dsem], 16)
        tok = slot.dcnt
        si = slot.dsem
        self.smax[si] = tok
        for b in r:
            b.r[si] = max(b.r.get(si, 0), tok)
        for b in w:
            if b.multi:
                b.w[si] = max(b.w.get(si, 0), tok)
            else:
                b.w = {si: tok}
                b.r = {}
        return ins

    def final_wait(self, bufs):
        need = {}
        for b in bufs:
            for si, v in b.w.items():
                need[si] = max(need.get(si, 0), v)
        for si, v in need.items():
            self.E["sp"].wait_ge(self.sems[si], v)


class Stream:
    def __init__(self, loads, depth):
        self.loads = loads
        self.nx = 0
        self.depth = depth

    def need(self, i):
        lim = min(i + self.depth, len(self.loads) - 1)
        while self.nx <= lim:
            self.loads[self.nx]()
            self.nx += 1


class Psum:
    def __init__(self, nc, stack):
        self.t = [stack.enter_context(nc.psum_tensor("psd%d" % i, [128, 1024], F32)) for i in range(4)]
        self.b = [Buf("ps%d" % i) for i in range(8)]
        self.i = 0

    def one(self):
        i = self.i
        self.i = (self.i + 1) % 8
        t = self.t[i // 2]
        return t[:, (i % 2) * 512:(i % 2) * 512 + 512], self.b[i]

    def two(self):
        if self.i % 2:
            self.i = (self.i + 1) % 8
        i = self.i
        self.i = (self.i + 2) % 8
        return self.t[i // 2][:, :], self.b[i], self.b[i + 1]


def bc(ap, shape):
    return ap.to_broadcast(shape)


def build(NCH):
    NCHP = NCH + 1
    LP = 128 * NCHP
    LPX = LP + 4
    NT = NCH * 128
    assert NCH >= 5
    nc = bass.Bass("TRN2", target_bir_lowering=False)
    dt_ = nc.dram_tensor

    def din(name, shape, dtype=F32):
        return dt_(name, list(shape), dtype, kind="ExternalInput").ap()

    def dsc(name, shape, dtype):
        return dt_(name, list(shape), dtype).ap()

    xin = din("xin", [LP, D])
    w_fm = din("w_fm", [68, 128, 16 * 128])
    w_tm = din("w_tm", [9, 128, 16 * 512])
    w_dt = din("w_dt", [128, 16 * 128])
    w_out = din("w_out", [24, 128, 8 * 512])
    w_gu = din("w_gu", [44, 128, 2 * 16 * 128])
    w_dn = din("w_dn", [44, 128, 4 * 512])
    gmix_i = din("gmix", [128, 16])
    gffn_i = din("gffn", [128, 16])
    qkg_i = din("qkg", [128, 2])
    rbl = din("rbl", [5, 32, 128, 640])
    msk = din("msk", [5, 128, 640])
    cw_i = din("cw", [128, 4 * 48])
    cb_i = din("cb", [1, 6144])
    dtb_i = din("dtb", [1, 128])
    alog_i = din("alog", [1, 128])
    dsk_i = din("dsk", [1, 64])
    nw_i = din("nw", [1, 4096])
    cst_i = din("cst", [128, 7 * 128])
    yout = dt_("y", [NT, D], F32, kind="ExternalOutput").ap()

    wb_fm = dsc("wb_fm", [68, 128, 2048], BF16)
    wb_tm = dsc("wb_tm", [9, 128, 8192], BF16)
    wb_dt = dsc("wb_dt", [128, 2048], BF16)
    wb_out = dsc("wb_out", [24, 128, 4096], BF16)
    wb_gu = dsc("wb_gu", [44, 128, 4096], BF16)
    wb_dn = dsc("wb_dn", [44, 128, 2048], BF16)
    qT = dsc("qT", [64, 32, LP], BF16)
    kT = dsc("kT", [64, 8, LP], BF16)
    vS = dsc("vS", [LP, 512], BF16)
    xbcT = dsc("xbcT", [48, 128, LPX], BF16)
    szS = dsc("szS", [LP, 4096], BF16)
    dtS = dsc("dtS", [LP, 128], F32)
    mixS = dsc("mixS", [LP, 6144], BF16)
    yfS = dsc("yfS", [LP, 4096], F32)
    ebf = dsc("ebf", [5, 128, 32, 640], BF16)
    cbD = dsc("cbD", [2, 6144], BF16)
    xsS = dsc("xsS", [LP, 4096], BF16)
    BtS = dsc("BtS", [LP, 1024], BF16)
    BCTS = dsc("BCTS", [NCHP, 128, 2048], BF16)

    B_wb = {n: Buf(n, multi=True) for n in ["fm", "tm", "dt", "out", "gu", "dn"]}
    B_qT, B_kT, B_vS, B_xbc, B_sz, B_dt, B_mix, B_yf, B_ebf, B_y = [
        Buf(n, multi=True) for n in ["qT", "kT", "vS", "xbc", "sz", "dtS", "mix", "yf", "ebf", "y"]]

    with ExitStack() as stack:
        k = K(nc, stack)
        PS = Psum(nc, stack)
        block = stack.enter_context(nc.Block())

        def sb(name, shape, dtype, st=None):
            return (st or stack).enter_context(nc.sbuf_tensor("s_" + name, list(shape), dtype))

        @block.sync
        def _(sync):
            cst = sb("cst", [128, 7, 128], F32)
            cstb = sb("cstb", [128, 7, 128], BF16)
            Bc = Buf("cst")
            k.dma("sp", cst[:].rearrange("p a b -> p (a b)"), cst_i, Bc, w=[Bc])
            k.op("dve", lambda e: e.tensor_copy(out=cstb[:], in_=cst[:]), r=[Bc], w=[Bc])
            ident = cstb[:, 0, :]
            TRI_LE, TRI_GE, SGT, SLT, ONESF, BDb = cst[:, 1, :], cst[:, 2, :], cst[:, 3, :], cst[:, 4, :], cst[:, 5, :], cstb[:, 6, :]
            MASKF, MASKB = cst[:, 1, :], cst[:, 2, :]
            epsT = sb("epsT", [128, 2], F32)
            k.op("dve", lambda e: e.memset(epsT[:, 0:1], EPS), w=[Bc])
            k.op("dve", lambda e: e.memset(epsT[:, 1:2], 1.0), w=[Bc])
            eps_ap, one_ap = epsT[:, 0:1], epsT[:, 1:2]
            gmix = sb("gmixt", [128, 16], F32)
            gffn = sb("gffnt", [128, 16], F32)
            qkg = sb("qkgt", [128, 2], F32)
            k.dma("sp", gmix[:], gmix_i, Bc, w=[Bc])
            k.dma("sp", gffn[:], gffn_i, Bc, w=[Bc])
            k.dma("sp", qkg[:], qkg_i, Bc, w=[Bc])
            k.op("dve", lambda e: e.tensor_scalar(out=qkg[:, 0:1], in0=qkg[:, 0:1], scalar1=0.125, scalar2=None, op0=ALU.mult), r=[Bc], w=[Bc])

            Bp0s = {n_: Buf("p0" + n_) for n_ in ["fm", "tm", "dt", "out", "gu", "dn"]}
            k.dma("pool", wb_dt, w_dt, Bp0s["dt"], w=[B_wb["dt"]])
            for (src, dst, key, n) in [(w_fm, wb_fm, "fm", 68), (w_tm, wb_tm, "tm", 9)]:
                for a in range(n):
                    k.dma("pool", dst[a], src[a], Bp0s[key], w=[B_wb[key]])

            def mk_cast(src, dst, key, a):
                return lambda: k.dma("pool", dst[a], src[a], Bp0s[key], w=[B_wb[key]])
            cast_tasks = [mk_cast(src, dst, key, a) for (src, dst, key, n) in [(w_out, wb_out, "out", 24), (w_gu, wb_gu, "gu", 44), (w_dn, wb_dn, "dn", 44)] for a in range(n)]

            def late_casts():
                while cast_tasks:
                    cast_tasks.pop(0)()

            with ExitStack() as s1:
                xb = [sb("xb%d" % i, [128, D], F32, s1) for i in range(2)]
                xbB = [Buf("xb%d" % i) for i in range(2)]
                junk = sb("junk", [128, D], BF16, s1)
                Bjunk = Buf("junk")
                xn = [sb("xn%d" % i, [128, D], BF16, s1) for i in range(2)]
                xnB = [Buf("xn%d" % i) for i in range(2)]
                st1 = sb("st1", [128, 8], F32, s1)
                Bst1 = [Buf("st1a"), Buf("st1b")]
                uT = [sb("uT%d" % i, [128, 16, 512], BF16, s1) for i in range(2)]
                uTB = [Buf("uT%d" % i) for i in range(2)]
                wfm = [sb("wfm%d" % i, [128, 16, 128], BF16, s1) for i in range(3)]
                wfmB = [Buf("wfm%d" % i) for i in range(3)]
                wtm = [sb("wtm%d" % i, [128, 16, 512], BF16, s1) for i in range(2)]
                wtmB = [Buf("wtm%d" % i) for i in range(2)]
                wdt = sb("wdt", [128, 16, 128], BF16, s1)
                Bwdt = Buf("wdt")
                k.dma("sp", wdt[:].rearrange("p a b -> p (a b)"), wb_dt, Bwdt, r=[B_wb["dt"]], w=[Bwdt])
                qf = [sb("qf%d" % i, [128, 512], F32, s1) for i in range(2)]
                qfB = [Buf("qf%d" % i) for i in range(2)]
                sq = [sb("sq%d" % i, [128, 512], BF16, s1) for i in range(2)]
                sqB = [Buf("sq%d" % i) for i in range(2)]
                rs = [sb("rs%d" % i, [128, 512], F32, s1) for i in range(2)]
                rsB = [Buf("rs%d" % i) for i in range(2)]
                ob = [sb("ob%d" % i, [128, 512], BF16, s1) for i in range(4)]
                obB = [Buf("ob%d" % i) for i in range(4)]
                of = [sb("of%d" % i, [128, 128], F32, s1) for i in range(2)]
                ofB = [Buf("of%d" % i) for i in range(2)]
                zt = sb("zt", [128, 4], BF16, s1)
                Bzt = Buf("zt")
                k.op("dve", lambda e: e.memset(zt[:], 0.0), w=[Bzt])
                k.dma("sp", xbcT[:, :, 0:2].rearrange("c p t -> p c t"), bc(zt[:, 0:2].unsqueeze(1), [128, 48, 2]), Bzt, r=[Bzt], w=[B_xbc])
                k.dma("sp", xbcT[:, :, LP + 2:LP + 4].rearrange("c p t -> p c t"), bc(zt[:, 0:2].unsqueeze(1), [128, 48, 2]), Bzt, r=[Bzt], w=[B_xbc])
                obi = 0
                ntile = (NCHP + 3) // 4
                et = [sb("et%d" % i, [128, 640], F32, s1) for i in range(2)]
                etB = [Buf("et%d" % i) for i in range(2)]
                mt = sb("mt", [128, 5, 640], F32, s1)
                Bmt = Buf("mt")
                k.dma("sp", mt[:], msk.rearrange("v p q -> p v q"), Bmt, w=[Bmt])
                eo = [sb("eo%d" % i, [128, 640], BF16, s1) for i in range(2)]
                eoB = [Buf("eo%d" % i) for i in range(2)]

                def mk_e(it):
                    var, h = it // 32, it % 32
                    i2 = it % 2

                    def f_load():
                        k.dma("sp", et[i2][:], rbl[var, h], etB[i2], w=[etB[i2]])

                    def f_comp():
                        k.op("act", lambda e: e.activation(out=et[i2][:], in_=et[i2][:], func=AF.Exp), r=[etB[i2]], w=[etB[i2]])
                        k.op("dve", lambda e: e.tensor_tensor(out=eo[i2][:], in0=et[i2][:], in1=mt[:, var, :], op=ALU.mult), r=[etB[i2], Bmt], w=[eoB[i2]])

                    def f_store():
                        k.dma("sp", ebf[var, :, h, :], eo[i2][:], eoB[i2], r=[eoB[i2]], w=[B_ebf])
                    return [f_load, f_comp, f_store]
                e_gap = max(1, ((ntile - 1) * 68) // 161)
                e_sched = {}
                for it in range(160):
                    parts = mk_e(it)
                    base = it * e_gap
                    offs = [0, max(1, e_gap // 2), max(2, e_gap // 2 + 1)]
                    for off, p_ in zip(offs, parts):
                        e_sched.setdefault(base + off, []).append((it, p_))
                e_keys = sorted(e_sched)
                e_pos = [0]

                def e_run(upto):
                    while e_pos[0] < len(e_keys) and e_keys[e_pos[0]] <= upto:
                        for _, p_ in e_sched[e_keys[e_pos[0]]]:
                            p_()
                        e_pos[0] += 1
                c_gap = max(1, ((ntile - 2) * 68) // 113)
                xn4 = [sb("xnq%d" % i, [128, D], BF16, s1) for i in range(2)]
                xn_ = xn + xn4
                xnB_ = xnB + [Buf("xnq%d" % i) for i in range(2)]
                st1b = sb("st1b", [128, 16], F32, s1)
                Bst4 = [Buf("st1_%d" % i) for i in range(4)]

                def mk_x(gi):
                    return lambda: k.dma("sp", xb[gi % 2][:], xin[gi * 128:(gi + 1) * 128, :], xbB[gi % 2], w=[xbB[gi % 2]])
                x_st = Stream([mk_x(gi) for gi in range(NCHP)], 1)

                def mk_fm(i):
                    return lambda: k.dma("sp", wfm[i % 3][:].rearrange("p a b -> p (a b)"), wb_fm[i % 68], wfmB[i % 3], r=[B_wb["fm"]], w=[wfmB[i % 3]])
                fm_st = Stream([mk_fm(i) for i in range(ntile * 68)], 2)

                def mk_tm(i):
                    return lambda: k.dma("sp", wtm[i % 2][:].rearrange("p a b -> p (a b)"), wb_tm[i % 9], wtmB[i % 2], r=[B_wb["tm"]], w=[wtmB[i % 2]])
                tm_st = Stream([mk_tm(i) for i in range(ntile * 9)], 1)

                def norm_stage(j):
                    c0 = 4 * j
                    for ci in range(min(4, NCHP - c0)):
                        gi = c0 + ci
                        x_st.need(gi)
                        x_ = xb[gi % 2]
                        xB = xbB[gi % 2]
                        sB = Bst4[gi % 4]
                        so = 4 * (gi % 4)
                        k.op("dve", lambda e: e.memset(st1b[:, so:so + 1], 0.0), w=[sB])
                        k.op("act", lambda e: e.activation(out=junk[:], in_=x_[:], func=AF.Square, accum_out=st1b[:, so:so + 1]), r=[xB, sB], w=[Bjunk, sB])
                        k.op("act", lambda e: e.activation(out=st1b[:, so + 1:so + 2], in_=st1b[:, so:so + 1], func=AF.Ln, bias=eps_ap, scale=1.0 / D), r=[sB, Bc], w=[sB])
                        k.op("act", lambda e: e.activation(out=st1b[:, so + 2:so + 3], in_=st1b[:, so + 1:so + 2], func=AF.Exp, scale=-0.5), r=[sB], w=[sB])
                        k.op("dve", lambda e: e.tensor_scalar(out=xn_[gi % 4][:], in0=x_[:], scalar1=st1b[:, so + 2:so + 3], scalar2=None, op0=ALU.mult), r=[xB, sB], w=[xnB_[gi % 4]])

                def tr_stage(j):
                    c0 = 4 * j
                    u = uT[j % 2]
                    uB = uTB[j % 2]
                    for ci in range(min(4, NCHP - c0)):
                        gi = c0 + ci
                        for half in range(2):
                            pt, pB = PS.one()
                            ptb = pt.bitcast(BF16).rearrange("p (a b) -> p a b", b=128)
                            for q8 in range(8):
                                kc = half * 8 + q8
                                k.op("pe", lambda e: e.transpose(out=ptb[:, q8, :], in_=xn_[gi % 4][:, kc * 128:(kc + 1) * 128], identity=ident),
                                     r=[xnB_[gi % 4], Bc], w=[pB], sig=(q8 == 7))
                            k.op("dve", lambda e: e.tensor_tensor(out=u[:, half * 8:half * 8 + 8, ci * 128:(ci + 1) * 128], in0=ptb[:, 0:8, :],
                                                                  in1=bc(gmix[:, half * 8:half * 8 + 8].unsqueeze(2), [128, 8, 128]), op=ALU.mult),
                                 r=[pB, Bc], w=[uB])

                def make_rest(ct, i2, o_, oB, ps2pair, TW, pp0):
                    def rest():
                        ps2, pB2 = PS.one()
                        k.op("pe", lambda e: e.matmul(ps2[:, :TW], lhsT=BDb, rhs=sq[i2][:, :TW], start=True, stop=True), r=[sqB[i2], Bc], w=[pB2])
                        k.op("act", lambda e: e.activation(out=rs[i2][:, :TW], in_=ps2[:, :TW], func=AF.Ln, bias=eps_ap, scale=1.0 / 64), r=[pB2, Bc], w=[rsB[i2]])
                        k.op("act", lambda e: e.activation(out=rs[i2][:, :TW], in_=rs[i2][:, :TW], func=AF.Exp, scale=-0.5), r=[rsB[i2]], w=[rsB[i2]])
                        gcol = qkg[:, 0:1] if ct < 16 else qkg[:, 1:2]
                        k.op("dve", lambda e: e.scalar_tensor_tensor(out=o_[:, :TW], in0=qf[i2][:, :TW], scalar=gcol, in1=rs[i2][:, :TW], op0=ALU.mult, op1=ALU.mult),
                             r=[qfB[i2], rsB[i2], Bc], w=[oB])
                        for hh in range(2):
                            if ct < 16:
                                dst = qT[:, 2 * ct + hh, pp0:pp0 + TW]
                                dB = B_qT
                            else:
                                dst = kT[:, 2 * (ct - 16) + hh, pp0:pp0 + TW]
                                dB = B_kT
                            k.dma("sp", dst, o_[hh * 64:(hh + 1) * 64, :TW], oB, r=[oB], w=[dB])
                    return rest

                norm_stage(0)
                tr_stage(0)
                for j in range(ntile):
                    if j == ntile - 1:
                        e_run(10 ** 9)
                    c0 = 4 * j
                    ncj = min(4, NCHP - c0)
                    TW = 128 * ncj
                    u = uT[j % 2]
                    uB = uTB[j % 2]
                    pp0 = c0 * 128
                    if j + 1 < ntile:
                        norm_stage(j + 1)
                    tm_st.need(j * 9 - 1)
                    deferred = None
                    for ct in range(68):
                        i = j * 68 + ct
                        fm_st.need(i)
                        wt = wfm[i % 3]
                        wB = wfmB[i % 3]
                        ps, pB = PS.one()
                        for kc in range(16):
                            k.op("pe", lambda e: e.matmul(ps[:, :TW], lhsT=wt[:, kc, :], rhs=u[:, kc, :TW], start=(kc == 0), stop=(kc == 15)),
                                 r=[wB, uB], w=[pB], sig=(kc == 15))
                        if deferred is not None:
                            deferred()
                            deferred = None
                        e_run(i)

                        o_ = ob[obi % 4]
                        oB = obB[obi % 4]
                        obi += 1
                        if ct < 20:
                            i2 = ct % 2
                            k.op("act", lambda e: e.activation(out=qf[i2][:, :TW], in_=ps[:, :TW], func=AF.Copy), r=[pB], w=[qfB[i2]])
                            k.op("pool", lambda e: e.tensor_tensor(out=sq[i2][:, :TW], in0=qf[i2][:, :TW], in1=qf[i2][:, :TW], op=ALU.mult), r=[qfB[i2]], w=[sqB[i2]])
                            deferred = make_rest(ct, i2, o_, oB, None, TW, pp0)
                        else:
                            if ct % 2:
                                k.op("act", lambda e: e.activation(out=o_[:, :TW], in_=ps[:, :TW], func=AF.Copy), r=[pB], w=[oB])
                            else:
                                k.op("dve", lambda e: e.tensor_copy(out=o_[:, :TW], in_=ps[:, :TW]), r=[pB], w=[oB])
                            k.dma("sp", xbcT[ct - 20, :, 2 + pp0:2 + pp0 + TW], o_[:, :TW], oB, r=[oB], w=[B_xbc])
                    if deferred is not None:
                        deferred()
                        deferred = None
                    if j + 1 < ntile:
                        tr_stage(j + 1)
                    for ng in range(9):
                        i = j * 9 + ng
                        tm_st.need(i)
                        wt = wtm[i % 2]
                        wB = wtmB[i % 2]
                        for ci in range(ncj):
                            ps, pB = PS.one()
                            for kc in range(16):
                                k.op("pe", lambda e: e.matmul(ps[:, :], lhsT=u[:, kc, ci * 128:(ci + 1) * 128], rhs=wt[:, kc, :], start=(kc == 0), stop=(kc == 15)),
                                     r=[wB, uB], w=[pB], sig=(kc == 15))
                            o_ = ob[obi % 4]
                            oB = obB[obi % 4]
                            obi += 1
                            r0 = pp0 + ci * 128
                            if ng == 0:
                                k.op("dve", lambda e: e.tensor_copy(out=o_[:], in_=ps[:, :]), r=[pB], w=[oB])
                                k.dma("sp", vS[r0:r0 + 128, :], o_[:], oB, r=[oB], w=[B_vS])
                            else:
                                k.op("act", lambda e: e.activation(out=o_[:], in_=ps[:, :], func=AF.Silu), r=[pB], w=[oB])
                                k.dma("sp", szS[r0:r0 + 128, (ng - 1) * 512:ng * 512], o_[:], oB, r=[oB], w=[B_sz])
                    for ci in range(ncj):
                        ps, pB = PS.one()
                        for kc in range(16):
                            k.op("pe", lambda e: e.matmul(ps[:, :128], lhsT=u[:, kc, ci * 128:(ci + 1) * 128], rhs=wdt[:, kc, :], start=(kc == 0), stop=(kc == 15)),
                                 r=[Bwdt, uB], w=[pB], sig=(kc == 15))
                        o_ = of[ci % 2]
                        oB = ofB[ci % 2]
                        r0 = pp0 + ci * 128
                        k.op("dve", lambda e: e.tensor_copy(out=o_[:], in_=ps[:, :128]), r=[pB], w=[oB])
                        k.dma("sp", dtS[r0:r0 + 128, :], o_[:], oB, r=[oB], w=[B_dt])

            k.barrier()
            with ExitStack() as s2:
                NKT = NCH
                Eint = sb("Eint", [128, 32, 640], BF16, s2)
                BEint = Buf("Eint")
                Eed = sb("Eed", [128, 32, 640], BF16, s2)
                BEed = Buf("Eed")
                k.dma("sp", Eint[:], ebf[0], BEint, r=[B_ebf], w=[BEint])
                kmT = sb("kmT", [64, 8, 16], BF16, s2)
                vm = sb("vm", [16, 8, 65], BF16, s2)
                Bm = Buf("meta")
                k.op("dve", lambda e: e.memset(vm[:], 1.0), w=[Bm])
                k.dma("sp", kmT[:], kT[:, :, 112:128], Bm, r=[B_kT], w=[Bm])
                k.dma("sp", vm[:, :, 0:64], vS[112:128, :].rearrange("p (g d) -> p g d", d=64), Bm, r=[B_vS], w=[Bm])
                Kb = [sb("Kb%d" % i, [64, 8, 640], BF16, s2) for i in range(2)]
                Qb = [sb("Qb%d" % i, [64, 32, 128], BF16, s2) for i in range(2)]
                Vb = [sb("Vb%d" % i, [128, 5, 8, 65], BF16, s2) for i in range(2)]
                KbB = [Buf("Kb%d" % i) for i in range(2)]
                QbB = [Buf("Qb%d" % i) for i in range(2)]
                VbB = [Buf("Vb%d" % i) for i in range(2)]
                for i in range(2):
                    k.op("pool", lambda e: e.memset(Vb[i][:], 1.0), w=[VbB[i]])
                PT = [sb("PT%d" % i, [128, 5, 512], BF16, s2) for i in range(2)]
                PTB = [Buf("PT%d" % i) for i in range(2)]
                PM = [sb("PM%d" % i, [16, 512], BF16, s2) for i in range(2)]
                PMB = [Buf("PM%d" % i) for i in range(2)]
                rc = [sb("rc%d" % i, [128, 4], F32, s2) for i in range(2)]
                rcB = [Buf("rc%d" % i) for i in range(2)]
                Ot = [sb("Ot%d" % i, [128, 8, 4, 64], BF16, s2) for i in range(2)]
                OtB = [Buf("Ot%d" % i) for i in range(2)]
                def R_info(R):
                    KT0 = min(max(R - 2, 0), NKT - 5)
                    if R == 0:
                        var = 1
                    elif R == 1:
                        var = 2
                    elif R == NKT - 2:
                        var = 3
                    elif R == NKT - 1:
                        var = 4
                    else:
                        var = 0
                    return KT0, var

                def mk_R(R):
                    def f():
                        KT0, var = R_info(R)
                        b0 = 128 * (1 + KT0)
                        i2 = R % 2
                        k.dma("sp", Kb[i2][:], kT[:, :, b0:b0 + 640], KbB[i2], r=[B_kT], w=[KbB[i2]])
                        k.dma("sp", Qb[i2][:], qT[:, :, 128 * (1 + R):128 * (2 + R)], QbB[i2], r=[B_qT], w=[QbB[i2]])
                        for kt in range(5):
                            k.dma("sp", Vb[i2][:, kt, :, 0:64], vS[b0 + kt * 128:b0 + (kt + 1) * 128, :].rearrange("p (g d) -> p g d", d=64), VbB[i2], r=[B_vS], w=[VbB[i2]])
                    return f
                R_st = Stream([mk_R(R) for R in range(NKT)], 1)
                for R in range(NKT):
                    R_st.need(R)
                    for _ in range(2):
                        if cast_tasks:
                            cast_tasks.pop(0)()
                    KT0, var = R_info(R)
                    i2 = R % 2
                    if var == 0:
                        Ec, EB = Eint, BEint
                    else:
                        k.dma("sp", Eed[:], ebf[var], BEed, r=[B_ebf], w=[BEed])
                        Ec, EB = Eed, BEed
                    O_ = Ot[i2]
                    OB = OtB[i2]

                    def qk(g):
                        p2 = g % 2
                        for kt in range(5):
                            ps, pB = PS.one()
                            k.op("pe", lambda e: e.matmul(ps[:, :], lhsT=Kb[i2][:, g, kt * 128:(kt + 1) * 128], rhs=Qb[i2][:, 4 * g:4 * g + 4, :].rearrange("p h q -> p (h q)"), start=True, stop=True),
                                 r=[KbB[i2], QbB[i2]], w=[pB])
                            k.op("act", lambda e: e.activation(out=PT[p2][:, kt, :], in_=ps[:, :], func=AF.Exp), r=[pB], w=[PTB[p2]])
                            k.op("dve" if kt < 4 else "pool", lambda e: e.tensor_tensor(out=PT[p2][:, kt, :].rearrange("p (h q) -> p h q", q=128), in0=PT[p2][:, kt, :].rearrange("p (h q) -> p h q", q=128),
                                                                  in1=Ec[:, 4 * g:4 * g + 4, kt * 128:(kt + 1) * 128], op=ALU.mult), r=[PTB[p2], EB], w=[PTB[p2]])
                        ps, pB = PS.one()
                        k.op("pe", lambda e: e.matmul(ps[0:16, :], lhsT=kmT[:, g, :], rhs=Qb[i2][:, 4 * g:4 * g + 4, :].rearrange("p h q -> p (h q)"), start=True, stop=True),
                             r=[Bm, QbB[i2]], w=[pB])
                        k.op("act", lambda e: e.activation(out=PM[p2][:], in_=ps[0:16, :], func=AF.Exp), r=[pB], w=[PMB[p2]])

                    def pv(g):
                        p2 = g % 2
                        po, poB = PS.one()
                        for h in range(4):
                            for kt in range(5):
                                k.op("pe", lambda e: e.matmul(po[:, h * 65:(h + 1) * 65], lhsT=PT[p2][:, kt, h * 128:(h + 1) * 128], rhs=Vb[i2][:, kt, g, :], start=(kt == 0), stop=False),
                                     r=[PTB[p2], VbB[i2]], w=[poB], sig=False)
                            k.op("pe", lambda e: e.matmul(po[:, h * 65:(h + 1) * 65], lhsT=PM[p2][:, h * 128:(h + 1) * 128], rhs=vm[:, g, :], start=False, stop=True),
                                 r=[PMB[p2], Bm], w=[poB], sig=(h == 3))
                        po3 = po[:, 0:260].rearrange("p (h d) -> p h d", d=65)
                        k.op("dve", lambda e: e.reciprocal(out=rc[p2][:], in_=po3[:, :, 64]), r=[poB], w=[rcB[p2]])
                        k.op("dve", lambda e: e.tensor_tensor(out=O_[:, g, :, :], in0=po3[:, :, 0:64], in1=bc(rc[p2][:].unsqueeze(2), [128, 4, 64]), op=ALU.mult),
                             r=[poB, rcB[p2]], w=[OB])

                    qk(0)
                    for g in range(8):
                        if g + 1 < 8:
                            qk(g + 1)
                        pv(g)
                    r0 = 128 * (1 + R)
                    k.dma("sp", mixS[r0:r0 + 128, 0:2048], O_[:].rearrange("p g h d -> p (g h d)"), OB, r=[OB], w=[B_mix])
                late_casts()

            k.barrier()
            with ExitStack() as s3:
                dtb = sb("dtb", [128, 128], F32, s3)
                abc = sb("abc", [128, 128], F32, s3)
                dsk = sb("dsk", [128, 64], F32, s3)
                nwl = [sb("nwl%d" % i, [128, 512], F32, s3) for i in range(2)]
                nwlB = [Buf("nwl%d" % i) for i in range(2)]
                Bpar = Buf("par")
                k.dma("sp", dtb[:], dtb_i.partition_broadcast(128), Bpar, w=[Bpar])
                k.dma("sp", abc[:], alog_i.partition_broadcast(128), Bpar, w=[Bpar])
                k.dma("sp", dsk[:], dsk_i.partition_broadcast(128), Bpar, w=[Bpar])
                k.op("act", lambda e: e.activation(out=abc[:], in_=abc[:], func=AF.Exp), r=[Bpar], w=[Bpar])
                k.op("dve", lambda e: e.tensor_scalar(out=abc[:], in0=abc[:], scalar1=-1.0, scalar2=None, op0=ALU.mult), r=[Bpar], w=[Bpar])

                xs0 = sb("xs0", [128, 64, 64], BF16, s3)
                xs = [xs0, xs0]
                xsB0 = Buf("xs0")
                xsB = [xsB0, xsB0]
                xdt0 = sb("xdt0", [128, 64, 64], BF16, s3)
                xdt = [xdt0, xdt0]
                xdtB0 = Buf("xdt0")
                xdtB = [xdtB0, xdtB0]
                Bt = [sb("Bt%d" % i, [128, 8, 128], BF16, s3) for i in range(2)]
                BtB = [Buf("Bt%d" % i) for i in range(2)]
                BCT = [sb("BCT%d" % i, [128, 16, 128], BF16, s3) for i in range(2)]
                BCTB = [Buf("BCT%d" % i) for i in range(2)]
                dtr = [sb("dtr%d" % i, [128, 128], F32, s3) for i in range(2)]
                dtrB = [Buf("dtr%d" % i) for i in range(2)]
                dtv = [sb("dtv%d" % i, [128, 64], F32, s3) for i in range(2)]
                dav = [sb("dav%d" % i, [128, 64], F32, s3) for i in range(2)]
                tmp1 = [sb("tmpa%d" % i, [128, 64], F32, s3) for i in range(2)]
                tmp2 = [sb("tmpb%d" % i, [128, 64], F32, s3) for i in range(2)]
                dvB = [Buf("dv%d" % i) for i in range(2)]
                eacs = [sb("eacs%d" % i, [128, 64], F32, s3) for i in range(2)]
                din_ = [sb("din%d" % i, [128, 64], F32, s3) for i in range(2)]
                elast = [sb("elast%d" % i, [128, 64], F32, s3) for i in range(2)]
                stB = [Buf("stt%d" % i) for i in range(2)]
                S = sb("S", [128, 8, 512], F32, s3)
                Sb_ = sb("Sb", [128, 8, 512], BF16, s3)
                SB = [Buf("S%d" % g) for g in range(8)]
                lseg = [sb("lseg%d" % i, [128, 8, 128], F32, s3) for i in range(3)]
                lsegB = [Buf("lseg%d" % i) for i in range(3)]
                LT = [sb("LT%d" % i, [128, 8, 128], BF16, s3) for i in range(3)]
                LTB = [Buf("LT%d" % i) for i in range(3)]
                CBm = [sb("CBm%d" % i, [128, 128], BF16, s3) for i in range(3)]
                CBmB = [Buf("CBm%d" % i) for i in range(3)]
                MT = [sb("MT%d" % i, [128, 8, 128], BF16, s3) for i in range(3)]
                MTB = [Buf("MT%d" % i) for i in range(3)]
                xdd = [sb("xdd%d" % i, [128, 8, 64], BF16, s3) for i in range(2)]
                xddB = [Buf("xdd%d" % i) for i in range(2)]
                t1 = [sb("t1%d" % i, [128, 8, 64], F32, s3) for i in range(2)]
                t1B = [Buf("t1%d" % i) for i in range(2)]
                yo = [sb("yo%d" % i, [128, 8, 64], F32, s3) for i in range(2)]
                yoB = [Buf("yo%d" % i) for i in range(2)]
                yfl = [sb("yfl%d" % i, [128, 512], F32, s3) for i in range(2)]
                yflB = [Buf("yfl%d" % i) for i in range(2)]
                szl = [sb("szl%d" % i, [128, 512], BF16, s3) for i in range(2)]
                szlB = [Buf("szl%d" % i) for i in range(2)]
                yg = [sb("yg%d" % i, [128, 512], F32, s3) for i in range(2)]
                ygB = [Buf("yg%d" % i) for i in range(2)]
                ss8 = [sb("ss8%d" % i, [128, 16], F32, s3) for i in range(2)]
                ss8B = [Buf("ss8%d" % i) for i in range(2)]
                jk3 = sb("jk3", [128, 512], BF16, s3)
                Bjk3 = Buf("jk3")
                yob = [sb("yob%d" % i, [128, 512], BF16, s3) for i in range(2)]
                yobB = [Buf("yob%d" % i) for i in range(2)]

                cnt = {"g": 0}

                def xdt_chunk(ci):
                    k.op("dve", lambda e: e.tensor_tensor(out=xdt[ci][:], in0=xs[ci][:], in1=bc(dtv[ci][:].unsqueeze(2), [128, 64, 64]), op=ALU.mult), r=[xsB[ci], dvB[ci]], w=xdtGG[ci])

                def conv_chunk(c, fwd, ci):
                    p_ = pre[ci]
                    pB_ = preB[ci]
                    for g in range(8):
                        ps, pB = PS.one()
                        k.op("pe", lambda e: e.matmul(ps[:, :], lhsT=ones1[:, :], rhs=cb2[:, g * 512:(g + 1) * 512], start=True, stop=False),
                             r=[Bcb], w=[pB], sig=False)
                        for q in range(4):
                            ct = 4 * g + q
                            for jj in range(4):
                                k.op("pe", lambda e: e.matmul(ps[:, q * 128:(q + 1) * 128], lhsT=p_[:, ct, jj:jj + 128], rhs=dg[:, ct, jj, :], start=False, stop=(jj == 3)),
                                     r=[pB_, Bdg], w=[pB], sig=(q == 3 and jj == 3))
                        k.op("act", lambda e: e.activation(out=xs[ci][:, 8 * g:8 * g + 8, :].rearrange("p h d -> p (h d)"), in_=ps[:, :], func=AF.Silu), r=[pB], w=[xsB[ci]])
                    for hb in range(2):
                        ps, pB = PS.one()
                        k.op("pe", lambda e: e.matmul(ps[:, :], lhsT=ones1[:, :], rhs=cb2[:, 4096 + hb * 512:4096 + (hb + 1) * 512], start=True, stop=False), r=[Bcb], w=[pB], sig=False)
                        for q in range(4):
                            ct = 32 + 4 * hb + q
                            for jj in range(4):
                                k.op("pe", lambda e: e.matmul(ps[:, q * 128:(q + 1) * 128], lhsT=p_[:, ct, jj:jj + 128], rhs=dg[:, ct, jj, :], start=False, stop=(jj == 3)),
                                     r=[pB_, Bdg], w=[pB], sig=(q == 3 and jj == 3))
                        k.op("act", lambda e: e.activation(out=Bt[ci][:, 4 * hb:4 * hb + 4, :].rearrange("p g n -> p (g n)"), in_=ps[:, :], func=AF.Silu), r=[pB], w=[BtB[ci]])
                    for hb in range(4):
                        ps, pB = PS.one()
                        for q in range(4):
                            ct = 32 + 4 * hb + q
                            k.op("pe", lambda e: e.matmul(ps[:, q * 128:(q + 1) * 128], lhsT=cb2[:, ct * 128:(ct + 1) * 128], rhs=ones1[:, :], start=True, stop=False), r=[Bcb], w=[pB], sig=False)
                            for jj in range(4):
                                k.op("pe", lambda e: e.matmul(ps[:, q * 128:(q + 1) * 128], lhsT=dg[:, ct, jj, :], rhs=p_[:, ct, jj:jj + 128], start=False, stop=(jj == 3)),
                                     r=[pB_, Bdg], w=[pB], sig=(q == 3 and jj == 3))
                        k.op("act", lambda e: e.activation(out=BCT[ci][:, 4 * hb:4 * hb + 4, :].rearrange("p g n -> p (g n)"), in_=ps[:, :], func=AF.Silu), r=[pB], w=[BCTB[ci]])

                def dt_chunk(c, fwd, ci):
                    off = 0 if fwd else 64
                    x_ = tmp1[ci]
                    B_ = dvB[ci]
                    k.op("dve", lambda e: e.tensor_tensor(out=x_[:], in0=dtr[ci][:, off:off + 64], in1=dtb[:, off:off + 64], op=ALU.add), r=[dtrB[ci], Bpar], w=[B_])
                    k.op("dve", lambda e: e.tensor_scalar(out=dtv[ci][:], in0=x_[:], scalar1=0.0, scalar2=None, op0=ALU.max), r=[B_], w=[B_])
                    k.op("dve", lambda e: e.tensor_scalar(out=tmp2[ci][:], in0=x_[:], scalar1=0.0, scalar2=None, op0=ALU.min), r=[B_], w=[B_])
                    k.op("dve", lambda e: e.tensor_tensor(out=tmp2[ci][:], in0=tmp2[ci][:], in1=dtv[ci][:], op=ALU.subtract), r=[B_], w=[B_])
                    k.op("act", lambda e: e.activation(out=tmp2[ci][:], in_=tmp2[ci][:], func=AF.Exp), r=[B_], w=[B_])
                    k.op("act", lambda e: e.activation(out=tmp2[ci][:], in_=tmp2[ci][:], func=AF.Ln, bias=one_ap, scale=1.0), r=[B_, Bc], w=[B_])
                    k.op("dve", lambda e: e.tensor_tensor(out=dtv[ci][:], in0=dtv[ci][:], in1=tmp2[ci][:], op=ALU.add), r=[B_], w=[B_])
                    if c == 0:
                        k.op("dve", lambda e: e.tensor_tensor(out=dtv[ci][:], in0=dtv[ci][:], in1=bc(cst[:, 2, 112:113], [128, 64]), op=ALU.mult), r=[Bc, B_], w=[B_])
                    k.op("dve", lambda e: e.tensor_tensor(out=dav[ci][:], in0=dtv[ci][:], in1=abc[:, off:off + 64], op=ALU.mult), r=[B_, Bpar], w=[B_])
                    ps, pB = PS.one()
                    k.op("pe", lambda e: e.matmul(ps[:, 0:64], lhsT=(TRI_LE if fwd else TRI_GE), rhs=dav[ci][:], start=True, stop=True), r=[B_, Bc], w=[pB], sig=False)
                    k.op("pe", lambda e: e.matmul(ps[:, 64:128], lhsT=(SGT if fwd else SLT), rhs=dav[ci][:], start=True, stop=True), r=[B_, Bc], w=[pB], sig=False)
                    k.op("pe", lambda e: e.matmul(ps[:, 128:192], lhsT=ONESF, rhs=dav[ci][:], start=True, stop=True), r=[B_, Bc], w=[pB])
                    k.op("act", lambda e: e.activation(out=eacs[ci][:], in_=ps[:, 0:64], func=AF.Exp), r=[pB], w=[stB[ci]])
                    k.op("act", lambda e: e.activation(out=din_[ci][:], in_=ps[:, 64:128], func=AF.Exp), r=[pB], w=[stB[ci]])
                    k.op("act", lambda e: e.activation(out=elast[ci][:], in_=ps[:, 128:192], func=AF.Exp), r=[pB], w=[stB[ci]])

                xdtGG = [[Buf("xdtG%d" % g) for g in range(8)]]
                xdtGG.append(xdtGG[0])
                tD = [sb("tD%d" % i, [128, 8, 64], BF16, s3) for i in range(3)]
                tDB = [Buf("tD%d" % i) for i in range(3)]

                def stageA(c, fwd, ci, g, first):
                    gi_ = g % 3
                    hs = slice(8 * g, 8 * g + 8)
                    k.op("pool", lambda e: e.tensor_tensor(out=lseg[gi_][:], in0=bc((TRI_LE if fwd else TRI_GE).unsqueeze(1), [128, 8, 128]),
                                                           in1=bc(dav[ci][:, hs].unsqueeze(2), [128, 8, 128]), op=ALU.mult), r=[dvB[ci], Bc], w=[lsegB[gi_]])
                    psg, pBa, pBb = PS.two()
                    for h2 in range(2):
                        k.op("pe", lambda e: e.matmul(psg[:, h2 * 512:(h2 + 1) * 512], lhsT=(SGT if fwd else SLT), rhs=lseg[gi_][:, 4 * h2:4 * h2 + 4, :].rearrange("p h l -> p (h l)"), start=True, stop=True),
                             r=[lsegB[gi_], Bc], w=[pBa, pBb], sig=(h2 == 1))
                    k.op("act", lambda e: e.activation(out=LT[gi_][:].rearrange("p h l -> p (h l)"), in_=psg[:, :], func=AF.Exp), r=[pBa, pBb], w=[LTB[gi_]])
                    pc, pcB = PS.one()
                    k.op("pe", lambda e: e.matmul(pc[:, 0:128], lhsT=BCT[ci][:, g, :], rhs=BCT[ci][:, 8 + g, :], start=True, stop=True), r=[BCTB[ci]], w=[pcB])
                    k.op("dve", lambda e: e.tensor_tensor(out=CBm[gi_][:], in0=pc[:, 0:128], in1=(MASKF if fwd else MASKB), op=ALU.mult), r=[pcB, Bc], w=[CBmB[gi_]])
                    k.op("dve", lambda e: e.tensor_tensor(out=MT[gi_][:], in0=LT[gi_][:], in1=bc(CBm[gi_][:].unsqueeze(1), [128, 8, 128]), op=ALU.mult), r=[LTB[gi_], CBmB[gi_]], w=[MTB[gi_]])
                    if fwd:
                        k.op("pool", lambda e: e.tensor_tensor(out=tD[gi_][:], in0=xs[ci][:, hs, :], in1=bc(dsk[:, hs].unsqueeze(2), [128, 8, 64]), op=ALU.mult), r=[xsB[ci], Bpar], w=[tDB[gi_]])

                def mk_xdt(ci, g):
                    hs = slice(8 * g, 8 * g + 8)
                    k.op("dve", lambda e: e.tensor_tensor(out=xdt[ci][:, hs, :], in0=xs[ci][:, hs, :], in1=bc(dtv[ci][:, hs].unsqueeze(2), [128, 8, 64]), op=ALU.mult), r=[xsB[ci], dvB[ci]], w=[xdtGG[ci][g]])

                def stageB(c, fwd, ci, g, need_y, first, m):
                    gi_ = g % 2
                    hs = slice(8 * g, 8 * g + 8)
                    if need_y:
                        py, pyB = PS.one()
                        if fwd:
                            k.op("pe", lambda e: e.matmul(py[:, :], lhsT=ident, rhs=tD[g % 3][:].rearrange("p h d -> p (h d)"), start=True, stop=False), r=[tDB[g % 3], Bc], w=[pyB], sig=False)
                        for h in range(8):
                            k.op("pe", lambda e: e.matmul(py[:, h * 64:(h + 1) * 64], lhsT=MT[g % 3][:, h, :], rhs=xdt[ci][:, 8 * g + h, :], start=(not fwd), stop=((not fwd) or h == 7)),
                                 r=[MTB[g % 3], xdtGG[ci][g]], w=[pyB], sig=(h == 7))
                        if not first:
                            pz, pzB = PS.one()
                            k.op("pe", lambda e: e.matmul(pz[:, :], lhsT=BCT[ci][:, 8 + g, :], rhs=Sb_[:, g, :], start=True, stop=True), r=[BCTB[ci], SB[g]], w=[pzB])
                    k.op("dve", lambda e: e.tensor_tensor(out=xdd[gi_][:], in0=xdt[ci][:, hs, :], in1=bc(din_[ci][:, hs].unsqueeze(2), [128, 8, 64]), op=ALU.mult), r=[xdtGG[ci][g], stB[ci]], w=[xddB[gi_]])
                    pi, piB = PS.one()
                    k.op("pe", lambda e: e.matmul(pi[:, :], lhsT=Bt[ci][:, g, :], rhs=xdd[gi_][:].rearrange("p h d -> p (h d)"), start=True, stop=True), r=[BtB[ci], xddB[gi_]], w=[piB])
                    if need_y:
                        if not first:
                            k.op("dve", lambda e: e.tensor_tensor(out=t1[gi_][:], in0=pz[:, :].rearrange("p (h d) -> p h d", d=64), in1=bc(eacs[ci][:, hs].unsqueeze(2), [128, 8, 64]), op=ALU.mult),
                                 r=[pzB, stB[ci]], w=[t1B[gi_]])
                            k.op("dve", lambda e: e.tensor_tensor(out=yo[gi_][:], in0=py[:, :].rearrange("p (h d) -> p h d", d=64), in1=t1[gi_][:], op=ALU.add), r=[pyB, t1B[gi_]], w=[yoB[gi_]])
                        else:
                            k.op("dve", lambda e: e.tensor_copy(out=yo[gi_][:], in_=py[:, :].rearrange("p (h d) -> p h d", d=64)), r=[pyB], w=[yoB[gi_]])
                        r0 = c * 128
                        if fwd:
                            k.dma("sp", yfS[r0:r0 + 128, g * 512:(g + 1) * 512], yo[gi_][:].rearrange("p h d -> p (h d)"), yoB[gi_], r=[yoB[gi_]], w=[B_yf])
                        else:
                            bg_st.need(m)
                            m2 = m % 2
                            k.op("dve", lambda e: e.tensor_tensor(out=yo[gi_][:].rearrange("p h d -> p (h d)"), in0=yo[gi_][:].rearrange("p h d -> p (h d)"), in1=yfl[m2][:], op=ALU.add), r=[yoB[gi_], yflB[m2]], w=[yoB[gi_]])
                            k.op("pool", lambda e: e.tensor_tensor(out=yg[m2][:], in0=yo[gi_][:].rearrange("p h d -> p (h d)"), in1=szl[m2][:], op=ALU.mult),
                                 r=[yoB[gi_], szlB[m2]], w=[ygB[m2]])
                            k.op("dve", lambda e: e.memset(ss8[m2][:, 0:1], 0.0), w=[ss8B[m2]])
                            k.op("act", lambda e: e.activation(out=jk3[:], in_=yg[m2][:], func=AF.Square, accum_out=ss8[m2][:, 0:1]), r=[ygB[m2], ss8B[m2]], w=[Bjk3, ss8B[m2]])
                            k.op("act", lambda e: e.activation(out=ss8[m2][:, 1:2], in_=ss8[m2][:, 0:1], func=AF.Ln, bias=eps_ap, scale=1.0 / 512), r=[ss8B[m2], Bc], w=[ss8B[m2]])
                            k.op("act", lambda e: e.activation(out=ss8[m2][:, 2:3], in_=ss8[m2][:, 1:2], func=AF.Exp, scale=-0.5), r=[ss8B[m2]], w=[ss8B[m2]])
                            k.op("dve", lambda e: e.scalar_tensor_tensor(out=yob[m2][:], in0=yg[m2][:], scalar=ss8[m2][:, 2:3], in1=nwl[m2][:], op0=ALU.mult, op1=ALU.mult),
                                 r=[ygB[m2], ss8B[m2], nwlB[m2]], w=[yobB[m2]])
                            k.dma("sp", mixS[r0:r0 + 128, 2048 + g * 512:2048 + (g + 1) * 512], yob[m2][:], yobB[m2], r=[yobB[m2]], w=[B_mix])
                    Sg = S[:, g, :].rearrange("p (h d) -> p h d", d=64)
                    if first:
                        k.op("dve", lambda e: e.tensor_copy(out=S[:, g, :], in_=pi[:, :]), r=[piB], w=[SB[g]])
                    else:
                        for h in range(8):
                            k.op("dve", lambda e: e.scalar_tensor_tensor(out=S[:, g, h * 64:(h + 1) * 64], in0=S[:, g, h * 64:(h + 1) * 64], scalar=elast[ci][:, 8 * g + h:8 * g + h + 1],
                                                                         in1=pi[:, h * 64:(h + 1) * 64], op0=ALU.mult, op1=ALU.add), r=[SB[g], stB[ci], piB], w=[SB[g]], noself=(h > 0))
                    k.op("act", lambda e: e.activation(out=Sb_[:, g, :], in_=S[:, g, :], func=AF.Copy), r=[SB[g]], w=[SB[g]])

                B_stash = Buf("stash", multi=True)
                s3f = ExitStack()
                s3f.__enter__()
                cw = sb("cw", [128, 4, 48], F32, s3f)
                Bcw = Buf("cw")
                k.dma("sp", cw[:].rearrange("p a b -> p (a b)"), cw_i, Bcw, w=[Bcw])
                dg = sb("dg", [128, 48, 4, 128], BF16, s3f)
                Bdg = Buf("dg")
                for ct in range(48):
                    for jj in range(4):
                        k.op("pool" if (ct % 2) else "dve", lambda e: e.tensor_scalar(out=dg[:, ct, jj, :], in0=cst[:, 0, :], scalar1=cw[:, jj, ct:ct + 1], scalar2=None, op0=ALU.mult),
                             r=[Bcw, Bc], w=[Bdg])
                cbf = sb("cbf", [48, 128], F32, s3f)
                cbh = sb("cbh", [48, 2, 128], BF16, s3f)
                cbt = sb("cbt", [48, 128], F32, s3f)
                Bcb0 = Buf("cb0")
                Bcb = Buf("cb")
                k.dma("sp", cbf[:], cb_i.rearrange("o (c p) -> (o c) p", p=128), Bcb0, w=[Bcb0])
                k.op("dve", lambda e: e.tensor_copy(out=cbh[:, 0, :], in_=cbf[:]), r=[Bcb0], w=[Bcb0])
                k.op("dve", lambda e: e.tensor_copy(out=cbt[:], in_=cbh[:, 0, :]), r=[Bcb0], w=[Bcb0])
                k.op("dve", lambda e: e.tensor_tensor(out=cbt[:], in0=cbf[:], in1=cbt[:], op=ALU.subtract), r=[Bcb0], w=[Bcb0])
                k.op("dve", lambda e: e.tensor_copy(out=cbh[:, 1, :], in_=cbt[:]), r=[Bcb0], w=[Bcb0])
                B_cbD = Buf("cbD", multi=True)
                k.dma("sp", cbD.rearrange("a (c p) -> c a p", p=128), cbh[:], Bcb0, r=[Bcb0], w=[B_cbD])
                cb2 = sb("cb2", [2, 6144], BF16, s3f)
                k.dma("sp", cb2[:], cbD, Bcb, r=[B_cbD], w=[Bcb])
                ones1 = sb("ones1", [2, 128], BF16, s3f)
                k.op("dve", lambda e: e.memset(ones1[:], 1.0), w=[Bcb])
                pre = [sb("pre%d" % i, [128, 48, 131], BF16, s3f) for i in range(2)]
                preB = [Buf("pre%d" % i) for i in range(2)]

                def mk_ckf(n):
                    return lambda: (k.dma("sp", pre[n % 2][:], xbcT[:, :, n * 128:n * 128 + 131].rearrange("c p t -> p c t"), preB[n % 2], r=[B_xbc], w=[preB[n % 2]]),
                                    k.dma("sp", dtr[n % 2][:], dtS[n * 128:(n + 1) * 128, :], dtrB[n % 2], r=[B_dt], w=[dtrB[n % 2]]))
                ckf_st = Stream([mk_ckf(n) for n in range(NCHP)], 1)

                for c in range(NCHP):
                    ci = c % 2
                    ckf_st.need(c)
                    first = (c == 0)
                    need_y = (c > 0)
                    conv_chunk(c, True, ci)
                    if c > 0:
                        r0 = c * 128
                        k.dma("sp", xsS[r0:r0 + 128, :], xs[ci][:].rearrange("p h d -> p (h d)"), xsB[ci], r=[xsB[ci]], w=[B_stash])
                        k.dma("sp", BtS[r0:r0 + 128, :], Bt[ci][:].rearrange("p g n -> p (g n)"), BtB[ci], r=[BtB[ci]], w=[B_stash])
                        k.dma("sp", BCTS[c], BCT[ci][:].rearrange("p g n -> p (g n)"), BCTB[ci], r=[BCTB[ci]], w=[B_stash])
                    dt_chunk(c, True, ci)
                    xdt_chunk(ci)
                    if need_y:
                        stageA(c, True, ci, 0, first)
                        stageA(c, True, ci, 1, first)
                    for g in range(8):
                        if need_y and g + 2 < 8:
                            stageA(c, True, ci, g + 2, first)
                        stageB(c, True, ci, g, need_y, first, 0)
                s3f.close()
                k.barrier()

                xs[1] = sb("xs_b", [128, 64, 64], BF16, s3)
                xsB[1] = Buf("xs_b")
                xdt[1] = sb("xdt_b", [128, 64, 64], BF16, s3)
                xdtGG[1] = [Buf("xdtGb%d" % g) for g in range(8)]
                nb_n = NCH

                def mk_ckb(nb):
                    c = NCHP - 1 - nb
                    bi = nb % 2
                    r0 = c * 128
                    return lambda: (k.dma("sp", xs[bi][:].rearrange("p h d -> p (h d)"), xsS[r0:r0 + 128, :], xsB[bi], r=[B_stash], w=[xsB[bi]]),
                                    k.dma("sp", Bt[bi][:].rearrange("p g n -> p (g n)"), BtS[r0:r0 + 128, :], BtB[bi], r=[B_stash], w=[BtB[bi]]),
                                    k.dma("sp", BCT[bi][:].rearrange("p g n -> p (g n)"), BCTS[c], BCTB[bi], r=[B_stash], w=[BCTB[bi]]),
                                    k.dma("sp", dtr[bi][:], dtS[r0:r0 + 128, :], dtrB[bi], r=[B_dt], w=[dtrB[bi]]))
                ckb = [mk_ckb(nb) for nb in range(nb_n)]

                def prologue(nb):
                    c = NCHP - 1 - nb
                    dt_chunk(c, False, nb % 2)
                    xdt_chunk(nb % 2)
                def mk_bg(m):
                    c = NCHP - 1 - (m // 8)
                    g = m % 8
                    r0 = c * 128
                    m2 = m % 2
                    return lambda: (k.dma("sp", yfl[m2][:], yfS[r0:r0 + 128, g * 512:(g + 1) * 512], yflB[m2], r=[B_yf], w=[yflB[m2]]),
                                    k.dma("sp", szl[m2][:], szS[r0:r0 + 128, g * 512:(g + 1) * 512], szlB[m2], r=[B_sz], w=[szlB[m2]]),
                                    k.dma("sp", nwl[m2][:], nw_i[:, g * 512:(g + 1) * 512].partition_broadcast(128), nwlB[m2], w=[nwlB[m2]]))
                bg_st = Stream([mk_bg(m) for m in range(NCH * 8)], 1)
                ckb[0]()
                prologue(0)
                for nb in range(nb_n):
                    c = NCHP - 1 - nb
                    ci = nb % 2
                    first = (nb == 0)
                    if nb + 1 < nb_n:
                        ckb[nb + 1]()
                    stageA(c, False, ci, 0, first)
                    stageA(c, False, ci, 1, first)
                    for g in range(8):
                        if g + 2 < 8:
                            stageA(c, False, ci, g + 2, first)
                        if g == 3 and nb + 1 < nb_n:
                            prologue(nb + 1)
                        stageB(c, False, ci, g, True, first, nb * 8 + g)

            k.barrier()
            with ExitStack() as s4:
                mixl = [sb("mixl%d" % i, [128, 6144], BF16, s4) for i in range(2)]
                mixlB = [Buf("mixl%d" % i) for i in range(2)]
                big = sb("big", [128, 48, 512], BF16, s4)
                Bbig = Buf("big")
                h1 = sb("h1", [128, 4, D], F32, s4)
                h1B = [Buf("h1_%d" % i) for i in range(4)]
                fn = [sb("fn%d" % i, [128, D], BF16, s4) for i in range(2)]
                fnB = [Buf("fn%d" % i) for i in range(2)]
                fT = sb("fT", [128, 16, 512], BF16, s4)
                BfT = Buf("fT")
                wo = [sb("wo%d" % i, [128, 8, 512], BF16, s4) for i in range(2)]
                woB = [Buf("wo%d" % i) for i in range(2)]
                wg = [sb("wg%d" % i, [128, 2, 16, 128], BF16, s4) for i in range(2)]
                wgB = [Buf("wg%d" % i) for i in range(2)]
                wd = [sb("wd%d" % i, [128, 4, 512], BF16, s4) for i in range(3)]
                wdB = [Buf("wd%d" % i) for i in range(3)]
                sg = [sb("sg%d" % i, [128, 512], F32, s4) for i in range(2)]
                sgB = [Buf("sg%d" % i) for i in range(2)]
                st4 = sb("st4", [128, 16], F32, s4)
                st4B = [Buf("st4_%d" % i) for i in range(4)]
                jk4 = sb("jk4", [128, D], BF16, s4)
                Bjk4 = Buf("jk4")
                ntile4 = (NCH + 3) // 4

                def mk_wo(i):
                    return lambda: k.dma("sp", wo[i % 2][:].rearrange("p a b -> p (a b)"), wb_out[i % 24], woB[i % 2], r=[B_wb["out"]], w=[woB[i % 2]])
                wo_st = Stream([mk_wo(i) for i in range(ntile4 * 24)], 1)

                def mk_wg(i):
                    return lambda: k.dma("sp", wg[i % 2][:].rearrange("p a b c -> p (a b c)"), wb_gu[i % 44], wgB[i % 2], r=[B_wb["gu"]], w=[wgB[i % 2]])
                wg_st = Stream([mk_wg(i) for i in range(ntile4 * 44)], 1)

                def mk_wd(i):
                    return lambda: k.dma("sp", wd[i % 3][:].rearrange("p a b -> p (a b)"), wb_dn[i % 44], wdB[i % 3], r=[B_wb["dn"]], w=[wdB[i % 3]])
                wd_st = Stream([mk_wd(i) for i in range(ntile4 * 44)], 2)
                def mk_ml(i):
                    return lambda: k.dma("sp", mixl[i % 2][:], mixS[(1 + i) * 128:(2 + i) * 128, :], mixlB[i % 2], r=[B_mix], w=[mixlB[i % 2]])
                ml_st = Stream([mk_ml(i) for i in range(NCH)], 1)
                for j in range(ntile4):
                    c0 = 1 + 4 * j
                    ncj = min(4, NCHP - c0)
                    TW = ncj * 128
                    wo_st.need(j * 24 - 1)
                    for ci in range(ncj):
                        gci = 4 * j + ci
                        ml_st.need(gci)
                        ml = mixl[gci % 2]
                        mB = mixlB[gci % 2]
                        for o8 in range(6):
                            pt, pB = PS.one()
                            ptb = pt.bitcast(BF16).rearrange("p (a b) -> p a b", b=128)
                            for q8 in range(8):
                                kc = o8 * 8 + q8
                                k.op("pe", lambda e: e.transpose(out=ptb[:, q8, :], in_=ml[:, kc * 128:(kc + 1) * 128], identity=ident), r=[mB, Bc], w=[pB], sig=(q8 == 7))
                            k.op("act" if o8 % 2 else "dve", (lambda e: e.activation(out=big[:, o8 * 8:o8 * 8 + 8, ci * 128:(ci + 1) * 128], in_=ptb[:, 0:8, :], func=AF.Copy)) if o8 % 2 else
                                 (lambda e: e.tensor_copy(out=big[:, o8 * 8:o8 * 8 + 8, ci * 128:(ci + 1) * 128], in_=ptb[:, 0:8, :])), r=[pB], w=[Bbig])
                    for ci in range(ncj):
                        r0 = (c0 + ci) * 128
                        k.dma("sp", h1[:, ci, :], xin[r0:r0 + 128, :], h1B[ci], w=[h1B[ci]])
                    for cg in range(4):
                        pss = [PS.one() for _ in range(ncj)]
                        for sub in range(6):
                            io = (j * 4 + cg) * 6 + sub
                            wo_st.need(io)
                            w_ = wo[io % 2]
                            wB = woB[io % 2]
                            for ci in range(ncj):
                                ps, pB = pss[ci]
                                for q8 in range(8):
                                    kc = sub * 8 + q8
                                    k.op("pe", lambda e: e.matmul(ps[:, :], lhsT=big[:, kc, ci * 128:(ci + 1) * 128], rhs=w_[:, q8, :], start=(kc == 0), stop=(kc == 47)),
                                         r=[Bbig, wB], w=[pB], sig=(q8 == 7))
                        for ci in range(ncj):
                            ps, pB = pss[ci]
                            k.op("dve", lambda e: e.tensor_tensor(out=h1[:, ci, cg * 512:(cg + 1) * 512], in0=ps[:, :], in1=h1[:, ci, cg * 512:(cg + 1) * 512], op=ALU.add),
                                 r=[pB, h1B[ci]], w=[h1B[ci]])
                    wg_st.need(j * 44 - 1)
                    for ci in range(ncj):
                        so = 4 * ci
                        sB = st4B[ci]
                        k.op("dve", lambda e: e.memset(st4[:, so:so + 1], 0.0), w=[sB])
                        k.op("act", lambda e: e.activation(out=jk4[:], in_=h1[:, ci, :], func=AF.Square, accum_out=st4[:, so:so + 1]), r=[h1B[ci], sB], w=[Bjk4, sB])
                        k.op("act", lambda e: e.activation(out=st4[:, so + 1:so + 2], in_=st4[:, so:so + 1], func=AF.Sqrt, bias=eps_ap, scale=1.0 / D), r=[sB, Bc], w=[sB])
                        k.op("dve", lambda e: e.reciprocal(out=st4[:, so + 2:so + 3], in_=st4[:, so + 1:so + 2]), r=[sB], w=[sB])
                        f_ = fn[ci % 2]
                        fB = fnB[ci % 2]
                        k.op("act", lambda e: e.activation(out=f_[:], in_=h1[:, ci, :], func=AF.Copy, scale=st4[:, so + 2:so + 3]), r=[h1B[ci], sB], w=[fB])
                        for half in range(2):
                            pt, pB = PS.one()
                            ptb = pt.bitcast(BF16).rearrange("p (a b) -> p a b", b=128)
                            for q8 in range(8):
                                kc = half * 8 + q8
                                k.op("pe", lambda e: e.transpose(out=ptb[:, q8, :], in_=f_[:, kc * 128:(kc + 1) * 128], identity=ident), r=[fB, Bc], w=[pB], sig=(q8 == 7))
                            k.op("dve", lambda e: e.tensor_tensor(out=fT[:, half * 8:half * 8 + 8, ci * 128:(ci + 1) * 128], in0=ptb[:, 0:8, :],
                                                                  in1=bc(gffn[:, half * 8:half * 8 + 8].unsqueeze(2), [128, 8, 128]), op=ALU.mult), r=[pB, Bc], w=[BfT])
                    for ht in range(44):
                        ig = j * 44 + ht
                        wg_st.need(ig)
                        w_ = wg[ig % 2]
                        wB = wgB[ig % 2]
                        if ht == 40:
                            wd_st.need(j * 44 - 1)
                        pg, pgB = PS.one()
                        pu, puB = PS.one()
                        for kc in range(16):
                            k.op("pe", lambda e: e.matmul(pg[:, :TW], lhsT=w_[:, 0, kc, :], rhs=fT[:, kc, :TW], start=(kc == 0), stop=(kc == 15)), r=[wB, BfT], w=[pgB], sig=(kc == 15))
                        for kc in range(16):
                            k.op("pe", lambda e: e.matmul(pu[:, :TW], lhsT=w_[:, 1, kc, :], rhs=fT[:, kc, :TW], start=(kc == 0), stop=(kc == 15)), r=[wB, BfT], w=[puB], sig=(kc == 15))
                        s_ = sg[ht % 2]
                        sB = sgB[ht % 2]
                        k.op("act", lambda e: e.activation(out=s_[:, :TW], in_=pg[:, :TW], func=AF.Silu), r=[pgB], w=[sB])
                        k.op("dve", lambda e: e.tensor_tensor(out=big[:, ht, :TW], in0=s_[:, :TW], in1=pu[:, :TW], op=ALU.mult), r=[sB, puB], w=[Bbig])
                    for cg in range(4):
                        pss = [PS.one() for _ in range(ncj)]
                        for sub in range(11):
                            idn = (j * 4 + cg) * 11 + sub
                            wd_st.need(idn)
                            if cg == 2 and sub == 0:
                                ml_st.need(4 * (j + 1) - 1)
                            w_ = wd[idn % 3]
                            wB = wdB[idn % 3]
                            for ci in range(ncj):
                                ps, pB = pss[ci]
                                for q4 in range(4):
                                    kc = sub * 4 + q4
                                    k.op("pe", lambda e: e.matmul(ps[:, :], lhsT=big[:, kc, ci * 128:(ci + 1) * 128], rhs=w_[:, q4, :], start=(kc == 0), stop=(kc == 43)),
                                         r=[Bbig, wB], w=[pB], sig=(q4 == 3))
                        for ci in range(ncj):
                            ps, pB = pss[ci]
                            k.op("dve", lambda e: e.tensor_tensor(out=h1[:, ci, cg * 512:(cg + 1) * 512], in0=ps[:, :], in1=h1[:, ci, cg * 512:(cg + 1) * 512], op=ALU.add),
                                 r=[pB, h1B[ci]], w=[h1B[ci]])
                    for ci in range(ncj):
                        r0 = (c0 - 1 + ci) * 128
                        k.dma("sp", yout[r0:r0 + 128, :], h1[:, ci, :], h1B[ci], r=[h1B[ci]], w=[B_y])
            k.final_wait([B_y])
    return nc


def _host_consts():
    c = np.zeros((128, 7, 128), np.float32)
    i = np.arange(128)
    c[:, 0, :] = np.eye(128)
    c[:, 1, :] = (i[:, None] <= i[None, :])
    c[:, 2, :] = (i[:, None] >= i[None, :])
    c[:, 3, :] = (i[:, None] > i[None, :])
    c[:, 4, :] = (i[:, None] < i[None, :])
    c[:, 5, :] = 1.0
    c[:, 6, :] = ((i[:, None] // 64) == (i[None, :] // 64))
    return c.reshape(128, 7 * 128)


def _rbl_tables(rpb, NCH):
    NR = 2 * NCH
    cols = np.arange(64)
    cs = np.clip(cols - 8, 0, 48)
    vars_R = [2, 0, 1, NCH - 2, NCH - 1]
    p = np.arange(128)
    i2, cp = p // 64, p % 64
    q = np.arange(128)
    j2, cq = q // 64, q % 64
    idx_dr = np.zeros((5, 128, 5, 128), np.int64)
    idx_dc = np.zeros((5, 128, 5, 128), np.int64)
    mask = np.zeros((5, 128, 5, 128), np.float32)
    for v, R in enumerate(vars_R):
        KT0 = min(max(R - 2, 0), NCH - 5)
        for kt in range(5):
            krow = 2 * KT0 + 2 * kt + i2[:, None]
            qrow = 2 * R + j2[None, :]
            rs = np.clip(qrow - 4, 0, NR - 8)
            dr = krow - qrow + 7
            okr = (krow >= rs) & (krow < rs + 8)
            dc = cp[:, None] - cq[None, :] + 15
            okc = (cp[:, None] >= cs[cq][None, :]) & (cp[:, None] < cs[cq][None, :] + 16)
            ok = okr & okc
            idx_dr[v, :, kt, :] = np.clip(dr, 0, 14)
            idx_dc[v, :, kt, :] = np.clip(dc, 0, 30)
            mask[v, :, kt, :] = ok
    rb = rpb[:, idx_dr, idx_dc]
    rb = np.ascontiguousarray(np.transpose(rb, (1, 0, 2, 3, 4))).reshape(5, 32, 128, 640)
    return rb.astype(np.float32), mask.reshape(5, 128, 640)


_CACHE = {}


def _prep_shared(NCH, meta_tokens, g_mix, w_in, q_norm, k_norm, rpb, conv_w, conv_b, dt_bias_f, dt_bias_b,
                 a_log_f, a_log_b, d_skip, ssd_norm, w_out, g_ffn, w_gate, w_up, w_down):
    f = np.float32
    w_in = np.asarray(w_in[0], f)
    fm = np.concatenate([w_in[:, 0:2560], w_in[:, 7168:13312]], axis=1)
    w_fm = np.ascontiguousarray(fm.reshape(16, 128, 68, 128).transpose(2, 1, 0, 3)).reshape(68, 128, 2048)
    tm = w_in[:, 2560:7168]
    w_tm = np.ascontiguousarray(tm.reshape(16, 128, 9, 512).transpose(2, 1, 0, 3)).reshape(9, 128, 8192)
    wd_ = w_in[:, 13312:13440]
    w_dt = np.ascontiguousarray(wd_.reshape(16, 128, 128).transpose(1, 0, 2)).reshape(128, 2048)
    wo = np.asarray(w_out[0], f)
    w_o = np.ascontiguousarray(wo.reshape(6, 8, 128, 4, 512).transpose(3, 0, 2, 1, 4)).reshape(24, 128, 4096)
    wg = np.asarray(w_gate[0], f).reshape(16, 128, 44, 128)
    wu = np.asarray(w_up[0], f).reshape(16, 128, 44, 128)
    w_gu = np.ascontiguousarray(np.stack([wg, wu], 0).transpose(3, 2, 0, 1, 4)).reshape(44, 128, 4096)
    wdn = np.asarray(w_down[0], f)
    w_dn = np.ascontiguousarray(wdn.reshape(11, 4, 128, 4, 512).transpose(3, 0, 2, 1, 4)).reshape(44, 128, 2048)
    rb, mk = _rbl_tables(np.asarray(rpb[0], f), NCH)
    cw = np.ascontiguousarray(np.asarray(conv_w[0], f).reshape(4, 48, 128).transpose(2, 0, 1)).reshape(128, 192)
    sh = {
        "w_fm": w_fm, "w_tm": w_tm, "w_dt": w_dt, "w_out": w_o, "w_gu": w_gu, "w_dn": w_dn,
        "gmix": np.ascontiguousarray(np.asarray(g_mix[0], f).reshape(16, 128).T),
        "gffn": np.ascontiguousarray(np.asarray(g_ffn[0], f).reshape(16, 128).T),
        "qkg": np.ascontiguousarray(np.stack([np.tile(np.asarray(q_norm[0], f), 2), np.tile(np.asarray(k_norm[0], f), 2)], 1)),
        "rbl": rb, "msk": mk, "cw": cw,
        "cb": np.asarray(conv_b[0], f).reshape(1, 6144),
        "dtb": np.concatenate([np.asarray(dt_bias_f[0], f), np.asarray(dt_bias_b[0], f)]).reshape(1, 128),
        "alog": np.concatenate([np.asarray(a_log_f[0], f), np.asarray(a_log_b[0], f)]).reshape(1, 128),
        "dsk": np.asarray(d_skip[0], f).reshape(1, 64),
        "nw": np.asarray(ssd_norm[0], f).reshape(1, 4096),
        "cst": _host_consts(),
    }
    return sh


def run_seqs(seqs, NCH, **params):
    meta = np.asarray(params["meta_tokens"], np.float32)
    sh = _prep_shared(NCH, **params)
    if NCH not in _CACHE:
        _CACHE[NCH] = build(NCH)
    nc = _CACHE[NCH]
    in_maps = []
    ncore = 8 if len(seqs) > 1 else 1
    for c in range(ncore):
        s = seqs[c % len(seqs)]
        xin = np.concatenate([np.zeros((112, D), np.float32), meta, np.asarray(s, np.float32)], axis=0)
        m = dict(sh)
        m["xin"] = np.ascontiguousarray(xin)
        in_maps.append(m)
    res = run_bass_kernel_spmd(nc, in_maps, core_ids=list(range(ncore)))
    return [np.asarray(res.results[c]["y"], np.float32) for c in range(len(seqs))]


def kernel(x_prompt, x_sample, meta_tokens, g_mix, w_in, q_norm, k_norm, rpb, conv_w, conv_b,
           dt_bias_f, dt_bias_b, a_log_f, a_log_b, d_skip, ssd_norm, w_out, g_ffn, w_gate, w_up, w_down):
    x_prompt = np.asarray(x_prompt, np.float32)
    x_sample = np.asarray(x_sample, np.float32)
    seqs = [x_prompt[i] for i in range(x_prompt.shape[0])] + [x_sample[i] for i in range(x_sample.shape[0])]
    NCH = seqs[0].shape[0] // 128
    outs = run_seqs(seqs, NCH, meta_tokens=meta_tokens, g_mix=g_mix, w_in=w_in, q_norm=q_norm, k_norm=k_norm, rpb=rpb,
                    conv_w=conv_w, conv_b=conv_b, dt_bias_f=dt_bias_f, dt_bias_b=dt_bias_b, a_log_f=a_log_f,
                    a_log_b=a_log_b, d_skip=d_skip, ssd_norm=ssd_norm, w_out=w_out, g_ffn=g_ffn, w_gate=w_gate,
                    w_up=w_up, w_down=w_down)
    nb = x_prompt.shape[0]
    y_prompt = np.stack(outs[:nb], 0)
    y_sample = np.stack(outs[nb:], 0)
    return (y_prompt, y_sample)
```
